# Optimizing a Trainium2 kernel written in Bass

```python
import jax, jax.numpy as jnp
from jax import lax
import numpy as np

D_MODEL = 1024
BATCH = 4
SEQ = 8192
DEPTH = 4

CHUNK = 64
D_MIX = D_MODEL
EPS = 1e-6

LRU_WIDTH = 3 * D_MIX // 8
LRU_HEAD_DIM = 64
LRU_HEADS = LRU_WIDTH // LRU_HEAD_DIM
CONV_WIDTH = 4
LRU_C = 8.0

MLA_HEADS = 6
MLA_NOPE = 64
MLA_ROPE = 32
MLA_V = 64
MLA_WIDTH = MLA_HEADS * MLA_V
Q_RANK = 192
KV_RANK = 128
ROPE_THETA = 10000.0
Q_BLOCK = 128

SGU_WIDTH = D_MIX - LRU_WIDTH - MLA_WIDTH
SGU_GROUPS = 4
SGU_GROUP_DIM = SGU_WIDTH // SGU_GROUPS
SGU_BLOCK = 128

IN_SPLITS = (LRU_WIDTH, LRU_WIDTH,
             Q_RANK, KV_RANK, MLA_ROPE, MLA_WIDTH,
             SGU_WIDTH, SGU_WIDTH, SGU_WIDTH)
D_IN = sum(IN_SPLITS)

kernel_name = "hybrid_rglru_mla_sgu_sandwich"


def rms_norm(x, g):
    xf = x.astype(jnp.float32)
    y = xf * lax.rsqrt(jnp.mean(xf * xf, axis=-1, keepdims=True) + EPS)
    return (y * g.astype(jnp.float32)).astype(x.dtype)


def layer_norm(x, g, b):
    xf = x.astype(jnp.float32)
    mu = jnp.mean(xf, axis=-1, keepdims=True)
    xc = xf - mu
    y = xc * lax.rsqrt(jnp.mean(xc * xc, axis=-1, keepdims=True) + EPS)
    return (y * g.astype(jnp.float32) + b.astype(jnp.float32)).astype(x.dtype)


def causal_depthwise_conv(x, w, b):
    y = lax.conv_general_dilated(
        x, w[:, None, :].astype(x.dtype), window_strides=(1,),
        padding=[(CONV_WIDTH - 1, 0)],
        dimension_numbers=('NWC', 'WIO', 'NWC'),
        feature_group_count=x.shape[-1])
    return y + b


def rg_lru(x, wa, ba, wx, bx, lam):
    B_, S_, _ = x.shape
    xh = x.reshape(B_, S_, LRU_HEADS, LRU_HEAD_DIM)
    gate_a = jax.nn.sigmoid(jnp.einsum('bshi,hij->bshj', xh, wa).reshape(B_, S_, LRU_WIDTH) + ba)
    gate_x = jax.nn.sigmoid(jnp.einsum('bshi,hij->bshj', xh, wx).reshape(B_, S_, LRU_WIDTH) + bx)
    log_a = -LRU_C * gate_a.astype(jnp.float32) * jax.nn.softplus(-lam.astype(jnp.float32))
    a = jnp.exp(log_a)
    mult = jnp.sqrt(-jnp.expm1(2.0 * log_a))
    b_in = mult * (gate_x * x).astype(jnp.float32)

    def combine(left, right):
        a_l, b_l = left
        a_r, b_r = right
        return a_l * a_r, a_r * b_l + b_r

    _, h = lax.associative_scan(combine, (a, b_in), axis=1)
    return h.astype(x.dtype)


def rope_cos_sin(positions):
    half = MLA_ROPE // 2
    inv_freq = ROPE_THETA ** (-jnp.arange(half, dtype=jnp.float32) / half)
    ang = positions.astype(jnp.float32)[..., None] * inv_freq
    return jnp.cos(ang), jnp.sin(ang)


def apply_rope(x, cos, sin):
    x1, x2 = jnp.split(x.astype(jnp.float32), 2, axis=-1)
    return jnp.concatenate([x1 * cos - x2 * sin, x2 * cos + x1 * sin], axis=-1).astype(x.dtype)


def mla(q_lat, kv_lat, k_rope, positions, q_norm_g, w_uq, kv_norm_g, w_ukv):
    B_, S_, _ = q_lat.shape
    d_qk = MLA_NOPE + MLA_ROPE
    q = (rms_norm(q_lat, q_norm_g) @ w_uq).reshape(B_, S_, MLA_HEADS, d_qk)
    kv = (rms_norm(kv_lat, kv_norm_g) @ w_ukv).reshape(B_, S_, MLA_HEADS, MLA_NOPE + MLA_V)
    q_nope, q_pe = q[..., :MLA_NOPE], q[..., MLA_NOPE:]
    k_nope, v = kv[..., :MLA_NOPE], kv[..., MLA_NOPE:]
    cos, sin = rope_cos_sin(positions)
    q_pe = apply_rope(q_pe, cos[:, :, None, :], sin[:, :, None, :])
    k_pe = apply_rope(k_rope, cos, sin)
    q = jnp.concatenate([q_nope, q_pe], axis=-1)
    k = jnp.concatenate([k_nope, jnp.broadcast_to(k_pe[:, :, None, :], (B_, S_, MLA_HEADS, MLA_ROPE))], axis=-1)
    scale = d_qk ** -0.5
    n_blk = S_ // Q_BLOCK
    q_blocks = q.reshape(B_, n_blk, Q_BLOCK, MLA_HEADS, d_qk).transpose(1, 0, 2, 3, 4)
    k_chunk = jnp.arange(S_) // CHUNK

    def attend(args):
        qb, blk = args
        q_chunk = (blk * Q_BLOCK + jnp.arange(Q_BLOCK)) // CHUNK
        s = jnp.einsum('bqhd,bkhd->bhqk', qb, k).astype(jnp.float32) * scale
        mask = k_chunk[None, :] <= q_chunk[:, None]
        s = jnp.where(mask[None, None], s, -jnp.inf)
        p = jax.nn.softmax(s, axis=-1).astype(v.dtype)
        return jnp.einsum('bhqk,bkhd->bqhd', p, v)

    o = lax.map(attend, (q_blocks, jnp.arange(n_blk)))
    return o.transpose(1, 0, 2, 3, 4).reshape(B_, S_, MLA_WIDTH)


def spatial_gating(u, v, norm_g, norm_b, w_s, b_s):
    B_, S_, _ = u.shape
    u = jax.nn.gelu(u)
    v = layer_norm(jax.nn.gelu(v), norm_g, norm_b)
    n_blk = S_ // SGU_BLOCK
    vb = v.reshape(B_, n_blk, SGU_BLOCK, SGU_GROUPS, SGU_GROUP_DIM)
    pos_chunk = jnp.arange(SGU_BLOCK) // CHUNK
    mask = pos_chunk[:, None] >= pos_chunk[None, :]
    w = jnp.where(mask[None], w_s, jnp.zeros_like(w_s))
    mixed = jnp.einsum('gij,bnjgc->bnigc', w, vb) + b_s.T[None, None, :, :, None]
    return u * mixed.reshape(B_, S_, SGU_WIDTH)


def hybrid_layer(x, positions, pre_g, w_in, conv_w, conv_b, wa, ba, wx, bx, lam,
                 q_norm_g, w_uq, kv_norm_g, w_ukv, sgu_g, sgu_bn, sgu_w, sgu_b,
                 branch_g, w_out, post_g):
    h = rms_norm(x, pre_g)
    proj = h @ w_in
    offsets = [int(o) for o in np.cumsum(IN_SPLITS)[:-1]]
    xa, ga, q_lat, kv_lat, k_rope, gb, u, v, gc = jnp.split(proj, offsets, axis=-1)
    ya = rg_lru(causal_depthwise_conv(xa, conv_w, conv_b), wa, ba, wx, bx, lam) * jax.nn.silu(ga)
    yb = mla(q_lat, kv_lat, k_rope, positions, q_norm_g, w_uq, kv_norm_g, w_ukv) * jax.nn.silu(gb)
    yc = spatial_gating(u, v, sgu_g, sgu_bn, sgu_w, sgu_b) * jax.nn.silu(gc)
    y = jnp.concatenate([
        rms_norm(ya, branch_g[:LRU_WIDTH]),
        rms_norm(yb, branch_g[LRU_WIDTH:LRU_WIDTH + MLA_WIDTH]),
        rms_norm(yc, branch_g[LRU_WIDTH + MLA_WIDTH:]),
    ], axis=-1)
    return x + rms_norm(y @ w_out, post_g)


def setup_inputs(seed: int = 0) -> dict:
    key = jax.random.key(seed)
    ks = jax.random.split(key, 24)
    f32 = jnp.float32

    def nrm(k, shape, scale):
        return jax.random.normal(k, shape, f32) * scale

    def gain(k, shape):
        return 1.0 + 0.05 * jax.random.normal(k, shape, f32)

    x = jax.random.normal(ks[0], (BATCH, SEQ, D_MODEL), f32)
    offset = jax.random.randint(ks[1], (BATCH, 1), 0, 4096, dtype=jnp.int32)
    positions = (offset + jnp.arange(SEQ, dtype=jnp.int32)[None, :]).astype(jnp.int32)
    a0 = jax.random.uniform(ks[2], (DEPTH, LRU_WIDTH), f32, minval=0.9, maxval=0.999)
    s0 = a0 ** (1.0 / LRU_C)
    lru_lambda = jnp.log(s0) - jnp.log1p(-s0)
    return {
        "x": x,
        "positions": positions,
        "pre_norm_g": gain(ks[3], (DEPTH, D_MODEL)),
        "w_in": nrm(ks[4], (DEPTH, D_MODEL, D_IN), D_MODEL ** -0.5),
        "conv_w": nrm(ks[5], (DEPTH, CONV_WIDTH, LRU_WIDTH), CONV_WIDTH ** -0.5),
        "conv_b": nrm(ks[6], (DEPTH, LRU_WIDTH), 0.02),
        "lru_wa": nrm(ks[7], (DEPTH, LRU_HEADS, LRU_HEAD_DIM, LRU_HEAD_DIM), LRU_HEAD_DIM ** -0.5),
        "lru_ba": nrm(ks[8], (DEPTH, LRU_WIDTH), 0.1),
        "lru_wx": nrm(ks[9], (DEPTH, LRU_HEADS, LRU_HEAD_DIM, LRU_HEAD_DIM), LRU_HEAD_DIM ** -0.5),
        "lru_bx": nrm(ks[10], (DEPTH, LRU_WIDTH), 0.1),
        "lru_lambda": lru_lambda,
        "q_norm_g": gain(ks[11], (DEPTH, Q_RANK)),
        "w_uq": nrm(ks[12], (DEPTH, Q_RANK, MLA_HEADS * (MLA_NOPE + MLA_ROPE)), Q_RANK ** -0.5),
        "kv_norm_g": gain(ks[13], (DEPTH, KV_RANK)),
        "w_ukv": nrm(ks[14], (DEPTH, KV_RANK, MLA_HEADS * (MLA_NOPE + MLA_V)), KV_RANK ** -0.5),
        "sgu_norm_g": gain(ks[15], (DEPTH, SGU_WIDTH)),
        "sgu_norm_b": nrm(ks[16], (DEPTH, SGU_WIDTH), 0.02),
        "sgu_w": nrm(ks[17], (DEPTH, SGU_GROUPS, SGU_BLOCK, SGU_BLOCK), SGU_BLOCK ** -0.5),
        "sgu_b": gain(ks[18], (DEPTH, SGU_GROUPS, SGU_BLOCK)),
        "branch_norm_g": gain(ks[19], (DEPTH, D_MIX)),
        "w_out": nrm(ks[20], (DEPTH, D_MIX, D_MODEL), D_MIX ** -0.5),
        "post_norm_g": gain(ks[21], (DEPTH, D_MODEL)),
    }


def reference(x, positions, pre_norm_g, w_in, conv_w, conv_b, lru_wa, lru_ba, lru_wx, lru_bx,
              lru_lambda, q_norm_g, w_uq, kv_norm_g, w_ukv, sgu_norm_g, sgu_norm_b, sgu_w, sgu_b,
              branch_norm_g, w_out, post_norm_g):
    h = x
    for l in range(DEPTH):
        h = hybrid_layer(h, positions, pre_norm_g[l], w_in[l], conv_w[l], conv_b[l],
                         lru_wa[l], lru_ba[l], lru_wx[l], lru_bx[l], lru_lambda[l],
                         q_norm_g[l], w_uq[l], kv_norm_g[l], w_ukv[l],
                         sgu_norm_g[l], sgu_norm_b[l], sgu_w[l], sgu_b[l],
                         branch_norm_g[l], w_out[l], post_norm_g[l])
    return h
```

```python
import numpy as np
import concourse.bass as bass
import concourse.mybir as mybir
from concourse.bass_utils import run_bass_kernel_spmd

F32 = mybir.dt.float32
BF16 = mybir.dt.bfloat16
I32 = mybir.dt.int32
AF = mybir.ActivationFunctionType
ALU = mybir.AluOpType
AX = mybir.AxisListType

D = 1024
DIN = 2272
LW = 384
NH = 6
DQK = 96
EPS = 1e-6
WCOLS = 2432
C_XA, C_GA, C_QL, C_KV, C_KR1, C_KR2, C_GB, C_U, C_V, C_GC = 0, 384, 768, 960, 1088, 1184, 1280, 1664, 1920, 2176


ATTACH_WAITS = True


class Buf:
    __slots__ = ("name", "t", "last_w", "readers", "sem", "semcnt")

    def __init__(self, name, t=None):
        self.name = name
        self.t = t
        self.last_w = None
        self.readers = []
        self.sem = None
        self.semcnt = 0

    def __getitem__(self, idx):
        return self.t[idx]


class Sched:
    def __init__(self, nc):
        self.nc = nc
        self.eng = {"pe": nc.tensor, "act": nc.scalar, "dve": nc.vector,
                    "pool": nc.gpsimd, "sp": nc.sync}
        self.sem = {k: nc.alloc_semaphore("prog_" + k) for k in self.eng}
        self.cnt = {k: 0 for k in self.eng}
        self.waited = {k: {} for k in self.eng}
        self.nwaits = 0
        self.nins = 0
        self.pe_mode = None
        self.uid = 0

    def sbuf(self, name, shape, dtype):
        return Buf(name, self.nc.alloc_sbuf_tensor(name, list(shape), dtype))

    def psum(self, name, shape, dtype=F32):
        return Buf(name, self.nc.alloc_psum_tensor(name, list(shape), dtype))

    def dram(self, name, shape, dtype, kind="Internal"):
        return Buf(name, self.nc.dram_tensor(name, list(shape), dtype, kind=kind))

    def view(self, name, t):
        return Buf(name, t)

    def _check(self, e, tok, same_engine_ok):
        if tok is None:
            return None
        sem, val, teng = tok
        if teng == e and same_engine_ok:
            return None
        key = id(sem)
        if self.waited[e].get(key, 0) >= val:
            return None
        self.waited[e][key] = val
        return (sem, val)

    def _need(self, e, tok, same_engine_ok):
        w = self._check(e, tok, same_engine_ok)
        if w is not None:
            self.eng[e].wait_ge(w[0], w[1])
            self.nwaits += 1

    def _collect(self, e, reads, writes, is_dma=False):
        out = []
        for i, b in enumerate(reads):
            w = self._check(e, b.last_w, False)
            if w is not None:
                out.append((w[0], w[1], i == 0))
        for b in writes:
            w = self._check(e, b.last_w, not is_dma)
            if w is not None:
                out.append((w[0], w[1], False))
            for r in b.readers:
                w = self._check(e, r, not is_dma)
                if w is not None:
                    out.append((w[0], w[1], False))
        return out

    def _deps(self, e, reads, writes, is_dma=False):
        for (sem, val, _) in self._collect(e, reads, writes, is_dma):
            self.eng[e].wait_ge(sem, val)
            self.nwaits += 1

    def _record(self, tok, reads, writes):
        for b in writes:
            b.last_w = tok
            b.readers = []
        for b in reads:
            b.readers.append(tok)
            if len(b.readers) > 16:
                best = {}
                for t in b.readers:
                    k = id(t[0])
                    if k not in best or best[k][1] < t[1]:
                        best[k] = t
                b.readers = list(best.values())

    def op(self, e, fn, reads=(), writes=(), mode=None, attach=True):
        if e == "pe":
            if mode != self.pe_mode and self.cnt["pe"] > 0:
                self._need("pe", (self.sem["pe"], self.cnt["pe"], "x"), False)
            self.pe_mode = mode
        waits = self._collect(e, reads, writes)
        att = None
        if attach and ATTACH_WAITS:
            for i in range(len(waits) - 1, -1, -1):
                if not (e == "pe" and waits[i][2]):
                    att = waits.pop(i)
                    break
        for (sem, val, _) in waits:
            self.eng[e].wait_ge(sem, val)
            self.nwaits += 1
        ins = fn()
        if att is not None:
            ins._wait_ge(att[0], att[1])
        self.cnt[e] += 1
        ins.then_inc(self.sem[e], 1)
        tok = (self.sem[e], self.cnt[e], e)
        self._record(tok, reads, writes)
        self.nins += 1
        return ins

    def dma(self, q, out_ap, in_ap, reads=(), writes=(), sembuf=None, **kw):
        self._deps(q, reads, writes, is_dma=True)
        sb = sembuf if sembuf is not None else (writes[0] if writes else reads[0])
        if sb.sem is None:
            self.uid += 1
            sb.sem = self.nc.alloc_semaphore("dma%d_%s" % (self.uid, sb.name))
        ins = self.eng[q].dma_start(out=out_ap, in_=in_ap, **kw)
        sb.semcnt += 16
        ins.then_inc(sb.sem, 16)
        tok = (sb.sem, sb.semcnt, "dma")
        self._record(tok, reads, writes)
        self.nins += 1
        return ins

    def finish(self, bufs, e="sp"):
        for b in bufs:
            self._need(e, b.last_w, False)


class Rot:
    def __init__(self, bufs):
        self.bufs = bufs
        self.i = 0
        self.held = {}
        self.owner = None

    def get(self):
        for _ in range(len(self.bufs)):
            b = self.bufs[self.i % len(self.bufs)]
            self.i += 1
            if id(b) not in self.held:
                if self.owner is not None:
                    self.held[id(b)] = self.owner
                return b
        raise RuntimeError("buffer pool exhausted")

    def release_owner(self, owner):
        self.held = {k: v for k, v in self.held.items() if v != owner}


def build(NT, NL, T):
    NB = T // 128
    SEQ = NT * T
    NKB = SEQ // 128
    nc = bass.Bass("TRN2", target_bir_lowering=False)
    S = Sched(nc)
    nv = nc.vector
    na = nc.scalar
    npool = nc.gpsimd
    npe = nc.tensor

    x_in = S.dram("x", [SEQ, D], F32, kind="ExternalInput")
    pos_in = S.dram("positions", [1, SEQ], I32, kind="ExternalInput")
    invf_in = S.dram("invf", [96, 1], F32, kind="ExternalInput")
    W = {}
    for nm, shp in [("pre_norm_g", [4, D]), ("w_in", [4, D, DIN]), ("conv_w", [4, 4, LW]), ("conv_b", [4, LW]),
                    ("lru_wa", [4, 6, 64, 64]), ("lru_ba", [4, LW]), ("lru_wx", [4, 6, 64, 64]), ("lru_bx", [4, LW]),
                    ("lru_lambda", [4, LW]), ("q_norm_g", [4, 192]), ("w_uq", [4, 192, 576]), ("kv_norm_g", [4, 128]),
                    ("w_ukv", [4, 128, 768]), ("sgu_norm_g", [4, 256]), ("sgu_norm_b", [4, 256]),
                    ("sgu_w", [4, 4, 128, 128]), ("sgu_b", [4, 4, 128]), ("branch_norm_g", [4, D]),
                    ("w_out", [4, D, D]), ("post_norm_g", [4, D])]:
        W[nm] = S.dram(nm, shp, F32, kind="ExternalInput")
    out_d = S.dram("out", [SEQ, D], F32, kind="ExternalOutput")
    xs_d = [S.dram("xs%d" % i, [SEQ, D], F32) for i in range(2)]
    tab_d = S.dram("tabs", [2, 96, SEQ], F32)
    KCH = 1024
    NCH = (SEQ + KCH - 1) // KCH
    kc_d = S.dram("kcache", [NL, 2, 96, NCH, 3, KCH], BF16)
    vc_d = S.dram("vcache", [NL, 2, 128, NKB, 3 * 65], BF16)
    xreg = {}
    def xr(key):
        if key not in xreg:
            xreg[key] = Buf("xr%s" % (key,))
        return xreg[key]
    tabreg = [Buf("tab%d" % j) for j in range(NT)]
    kreg = {}
    vreg = {}

    ident = S.sbuf("ident", [128, 128], BF16)
    ones = S.sbuf("ones", [128, 128], BF16)
    esel = S.sbuf("esel", [128, 128], BF16)
    invf = S.sbuf("invf_sb", [96, 1], F32)
    win = S.sbuf("win", [128, 8, WCOLS], BF16)
    wout = S.sbuf("wout", [128, 8, D], BF16)
    wuq = S.sbuf("wuq", [96, 2, 576], BF16)
    wuqr = S.sbuf("wuqr", [96, 2, 576], BF16)
    wukv = S.sbuf("wukv", [128, 800], BF16)
    wv = S.sbuf("wv", [128, 384], BF16)
    wabd = S.sbuf("wabd", [128, 3, 128], BF16)
    wxbd = S.sbuf("wxbd", [128, 3, 128], BF16)
    wsT = S.sbuf("wsT", [128, 4, 128], BF16)
    wstmp = S.sbuf("wstmp", [128, 4, 128], BF16)
    pv = S.sbuf("pvec", [128, 64], F32)
    PV_GPRE, PV_BG, PV_CW, PV_CB, PV_BA, PV_BX, PV_LAM, PV_CC, PV_QG, PV_KVG, PV_TMP = 0, 8, 16, 28, 31, 34, 37, 40, 46, 48, 50
    PV_NBA, PV_NBX = 54, 57
    gpost = S.sbuf("gpost", [128, D], F32)
    sgn_g = S.sbuf("sgn_g", [128, 256], F32)
    sgn_b = S.sbuf("sgn_b", [128, 256], F32)
    bsbc = S.sbuf("bsbc", [128, 2, T], F32)
    xt_pool = Rot([S.sbuf("xt%d" % i, [128, NB * D], F32) for i in range(2)])
    stg_pool = Rot([S.sbuf("stg%d" % i, [128, 1152], F32) for i in range(2)])
    hb_pool = Rot([S.sbuf("hb%d" % i, [128, D], BF16) for i in range(4)])
    hT = S.sbuf("hT", [128, 8, T], BF16)
    yT = S.sbuf("yT", [128, 8, T], BF16)
    yf = S.sbuf("yf", [128, 8, T], F32)
    sgb = S.sbuf("sgb", [128, 3, T], F32)
    ugc = S.sbuf("ugc", [128, 2, T], F32)
    xa_sb3 = [S.sbuf("xa_sb%d" % i, [128, T + 3], F32) for i in range(3)]
    hs_pool = Rot([S.sbuf("hs%d" % i, [128, 3, T], F32) for i in range(2)])
    ql = S.sbuf("ql", [96, 2, T], F32)
    qln = S.sbuf("qln", [96, 2, T], BF16)
    qT = S.sbuf("qT", [96, NH, T], BF16)
    kcur = S.sbuf("kcur", [96, NH, T], BF16)
    vcur = S.sbuf("vcur", [128, NB, NH, 65], BF16)
    ckvT = S.sbuf("ckvT", [128, T], BF16)
    qT_h = [Buf("qT_h%d" % h, qT.t) for h in range(NH)]
    yT_c = [Buf("yT_c%d" % i, yT.t) for i in range(2)]
    yT_b = Buf("yT_b", yT.t)
    yfA, yfB, yfC = Buf("yfA", yf.t), Buf("yfB", yf.t), Buf("yfC", yf.t)
    kcur_h = [Buf("kcur_h%d" % h, kcur.t) for h in range(NH)]
    vcur_b = [Buf("vcur_b%d" % b, vcur.t) for b in range(NB)]
    zhi3 = [S.sbuf("zhi_%d" % i, [128, T], BF16) for i in range(3)]
    zlo3 = [S.sbuf("zlo_%d" % i, [128, T], BF16) for i in range(3)]
    vnA2 = [S.sbuf("vnA_%d" % i, [128, 256], BF16) for i in range(NB)]
    vnB2 = [S.sbuf("vnB_%d" % i, [128, 256], BF16) for i in range(NB)]
    kpe = S.sbuf("kpe", [96, T], BF16)
    ctab = S.sbuf("ctab", [96, T], F32)
    stab = S.sbuf("stab", [96, T], F32)
    gvb = S.sbuf("gvb", [128, NB, 256], F32)
    kb_pool = Rot([S.sbuf("kbuf%d" % i, [96, 3, KCH], BF16) for i in range(2)])
    vb_pool = Rot([S.sbuf("vbuf%d" % i, [128, KCH // 128, 3 * 65], BF16) for i in range(2)])
    Tf = Rot([S.sbuf("tf%d" % i, [128, T], F32) for i in range(17)])
    Tb = Rot([S.sbuf("tb%d" % i, [128, T], BF16) for i in range(6)])
    Pp = Rot([S.sbuf("pT%d" % i, [128, 1024], BF16) for i in range(3)])
    st_pool = Rot([S.sbuf("st%d" % i, [128, 16], F32) for i in range(6)])
    psw_s = nc.alloc_psum_tensor("psw_s", [128, 1024], F32)
    ps_s = [Buf("ps_s%d" % i, psw_s[:, i * 512:(i + 1) * 512]) for i in range(2)]
    ps_o = [S.psum("ps_o%d" % i, [128, 512]) for i in range(3)]
    psw_g = nc.alloc_psum_tensor("psw_g", [128, 1024], F32)
    ps_g = [Buf("ps_g%d" % i, psw_g[:, i * 512:(i + 1) * 512]) for i in range(2)]
    PW = [(psw_g, ps_g[0], ps_g[1]), (psw_s, ps_s[0], ps_s[1])]
    ps_t = S.psum("ps_t", [128, 1024], BF16)
    PG = Rot(ps_g + ps_s)
    PSr = Rot(ps_s)
    PS4 = Rot(ps_g + ps_s)

    def ACT(out, in_, func, reads, writes, **kw):
        return S.op("act", lambda: na.activation(out=out, in_=in_, func=func, **kw), reads, writes,
                    attach=("accum_out" not in kw))

    def TT(e, out, in0, in1, op, reads, writes):
        eng = nv if e == "dve" else npool
        return S.op(e, lambda: eng.tensor_tensor(out=out, in0=in0, in1=in1, op=op), reads, writes)

    def TS(e, out, in0, s1, s2, op0, op1, reads, writes):
        eng = nv if e == "dve" else npool
        if s2 is None:
            return S.op(e, lambda: eng.tensor_scalar(out=out, in0=in0, scalar1=s1, scalar2=None, op0=op0), reads, writes)
        return S.op(e, lambda: eng.tensor_scalar(out=out, in0=in0, scalar1=s1, scalar2=s2, op0=op0, op1=op1), reads, writes)

    def STT(out, in0, sc, in1, op0, op1, reads, writes):
        return S.op("dve", lambda: nv.scalar_tensor_tensor(out=out, in0=in0, scalar=sc, in1=in1, op0=op0, op1=op1), reads, writes)

    def CP(e, out, in_, reads, writes):
        if e == "act":
            return S.op("act", lambda: na.copy(out=out, in_=in_), reads, writes)
        eng = nv if e == "dve" else npool
        return S.op(e, lambda: eng.tensor_copy(out=out, in_=in_), reads, writes)

    def MM(out, lhsT, rhs, start, stop, reads, writes):
        return S.op("pe", lambda: npe.matmul(out, lhsT=lhsT, rhs=rhs, start=start, stop=stop), reads, writes)

    def MSET(e, ap, val, writes):
        eng = nv if e == "dve" else npool
        return S.op(e, lambda: eng.memset(ap, val), (), writes)

    def rstd_from(ps_ap, n, out_ap, reads, writes):
        ACT(out_ap, ps_ap, AF.Ln, reads, writes, scale=1.0 / n, bias=EPS)
        ACT(out_ap, out_ap, AF.Exp, writes, writes, scale=-0.5)

    tmpf = Tf.get()
    MSET("pool", tmpf[:, 0:128], 0.0, [tmpf])
    S.op("pool", lambda: npool.affine_select(out=tmpf[:, 0:128], in_=tmpf[:, 0:128], pattern=[[-1, 128]],
                                             compare_op=ALU.not_equal, fill=1.0, base=0, channel_multiplier=1),
         [tmpf], [tmpf])
    CP("dve", ident[:], tmpf[:, 0:128], [tmpf], [ident])
    MSET("dve", ones[:], 1.0, [ones])
    MSET("dve", esel[:], 0.0, [esel])
    MSET("dve", esel[64:65, :], 1.0, [esel])
    for i3 in range(3):
        MSET("pool", zhi3[i3][:], 0.0, [zhi3[i3]])
        MSET("pool", zlo3[i3][:], 0.0, [zlo3[i3]])
    MSET("pool", vcur[:], 1.0, vcur_b)
    MSET("pool", win[:], 0.0, [win])
    MSET("pool", wuqr[:], 0.0, [wuqr])
    MSET("pool", wukv[:], 0.0, [wukv])
    for i3 in range(NB):
        MSET("dve", vnA2[i3][:], 0.0, [vnA2[i3]])
        MSET("dve", vnB2[i3][:], 0.0, [vnB2[i3]])
    S.dma("sp", invf[:], invf_in[:], reads=[invf_in], writes=[invf])

    def run_rr(gens, W, pools=(), stagger=False):
        gens = list(gens)
        active = []
        while gens or active:
            if stagger:
                if gens and len(active) < W:
                    active.append(gens.pop(0))
            else:
                while gens and len(active) < W:
                    active.append(gens.pop(0))
            for g in list(active):
                for p in pools:
                    p.owner = id(g)
                try:
                    next(g)
                except StopIteration:
                    active.remove(g)
                    for p in pools:
                        p.release_owner(id(g))
                for p in pools:
                    p.owner = None

    C1 = 6.28125
    C2 = 2.0 * np.pi - 6.28125
    def g_tab(j, which):
        b0, b1, b2 = Tf.get(), Tf.get(), Tf.get()
        S.dma("sp", b0[0:96, :].bitcast(I32), pos_in[0:1, j * T:(j + 1) * T].to_broadcast([96, T]), reads=[pos_in], writes=[b0])
        yield
        CP("dve", b1[0:96, :], b0[0:96, :].bitcast(I32), [b0], [b1])
        yield
        if which == 0:
            TS("dve", b1[0:96, :], b1[0:96, :], invf[:, 0:1], float(np.pi / 2), ALU.mult, ALU.add, [b1, invf], [b1])
        else:
            TS("dve", b1[0:96, :], b1[0:96, :], invf[:, 0:1], None, ALU.mult, None, [b1, invf], [b1])
        yield
        TS("dve", b0[0:96, :].bitcast(I32), b1[0:96, :], float(1.0 / (2 * np.pi)), None, ALU.mult, None, [b1], [b0])
        yield
        CP("dve", b2[0:96, :], b0[0:96, :].bitcast(I32), [b0], [b2])
        yield
        STT(b0[0:96, :], b2[0:96, :], -C1, b1[0:96, :], ALU.mult, ALU.add, [b2, b1], [b0])
        yield
        STT(b1[0:96, :], b2[0:96, :], -C2, b0[0:96, :], ALU.mult, ALU.add, [b2, b0], [b1])
        yield
        TS("dve", b1[0:96, :], b1[0:96, :], 3.1415925, -3.1415925, ALU.min, ALU.max, [b1], [b1])
        yield
        ACT(b2[0:96, :], b1[0:96, :], AF.Sin, [b1], [b2])
        yield
        S.dma("sp", tab_d[which, :, j * T:(j + 1) * T], b2[0:96, :], reads=[b2], writes=[tabreg[j]], sembuf=b2)
        yield

    run_rr([g_tab(j, w_) for j in range(NT) for w_ in range(2)], 5, [Tf], stagger=True)

    def load_vec(dst_ap, src_ap, reads):
        S.dma("sp", dst_ap, src_ap, reads=reads, writes=[pv], allow_slow_non_contiguous=True)

    def prep_layer(l):
        load_vec(pv[:, PV_GPRE:PV_GPRE + 8], W["pre_norm_g"][l, :].rearrange("(c p) -> p c", p=128), [W["pre_norm_g"]])
        load_vec(pv[:, PV_BG:PV_BG + 8], W["branch_norm_g"][l, :].rearrange("(c p) -> p c", p=128), [W["branch_norm_g"]])
        for k in range(4):
            load_vec(pv[:, PV_CW + 3 * k:PV_CW + 3 * k + 3], W["conv_w"][l, k, :].rearrange("(c p) -> p c", p=128), [W["conv_w"]])
        for nm, off in [("conv_b", PV_CB), ("lru_ba", PV_BA), ("lru_bx", PV_BX), ("lru_lambda", PV_LAM)]:
            load_vec(pv[:, off:off + 3], W[nm][l, :].rearrange("(c p) -> p c", p=128), [W[nm]])
        load_vec(pv[0:96, PV_QG:PV_QG + 2], W["q_norm_g"][l, :].rearrange("(c p) -> p c", p=96), [W["q_norm_g"]])
        load_vec(pv[:, PV_KVG:PV_KVG + 1], W["kv_norm_g"][l, :].rearrange("(c p) -> p c", p=128), [W["kv_norm_g"]])
        TS("dve", pv[:, PV_NBA:PV_NBA + 3], pv[:, PV_BA:PV_BA + 3], -1.0, None, ALU.mult, None, [pv], [pv])
        TS("dve", pv[:, PV_NBX:PV_NBX + 3], pv[:, PV_BX:PV_BX + 3], -1.0, None, ALU.mult, None, [pv], [pv])
        ACT(pv[:, PV_TMP:PV_TMP + 3], pv[:, PV_LAM:PV_LAM + 3], AF.Exp, [pv], [pv], scale=-1.0)
        ACT(pv[:, PV_TMP:PV_TMP + 3], pv[:, PV_TMP:PV_TMP + 3], AF.Ln, [pv], [pv], bias=1.0)
        TS("dve", pv[:, PV_CC:PV_CC + 3], pv[:, PV_TMP:PV_TMP + 3], -8.0, None, ALU.mult, None, [pv], [pv])
        TS("dve", pv[:, PV_CC + 3:PV_CC + 6], pv[:, PV_TMP:PV_TMP + 3], -16.0, None, ALU.mult, None, [pv], [pv])
        S.dma("sp", gpost[:], W["post_norm_g"][l:l + 1, :].to_broadcast([128, D]), reads=[W["post_norm_g"]], writes=[gpost])
        S.dma("sp", sgn_g[:], W["sgu_norm_g"][l:l + 1, :].to_broadcast([128, 256]), reads=[W["sgu_norm_g"]], writes=[sgn_g])
        S.dma("sp", sgn_b[:], W["sgu_norm_b"][l:l + 1, :].to_broadcast([128, 256]), reads=[W["sgu_norm_b"]], writes=[sgn_b])
        for cc in range(2):
            for gi in range(2):
                for b in range(NB):
                    S.dma("sp", bsbc[gi * 64:(gi + 1) * 64, cc, b * 128:(b + 1) * 128],
                          W["sgu_b"][l, 2 * cc + gi:2 * cc + gi + 1, :].to_broadcast([64, 128]),
                          reads=[W["sgu_b"]], writes=[bsbc])
        for c in range(8):
            g = pv[:, PV_GPRE + c:PV_GPRE + c + 1]
            e = "dve"
            stg = stg_pool.get()
            S.dma("sp", stg[:, 0:1120], W["w_in"][l, c * 128:(c + 1) * 128, 0:1120], reads=[W["w_in"]], writes=[stg])
            TS(e, win[:, c, 0:1088], stg[:, 0:1088], g, None, ALU.mult, None, [stg, pv], [win])
            TS(e, win[:, c, C_KR1 + 64:C_KR1 + 96], stg[:, 1088:1120], g, None, ALU.mult, None, [stg, pv], [win])
            TS(e, win[:, c, C_KR2 + 64:C_KR2 + 80], stg[:, 1104:1120], g, -1.0, ALU.mult, ALU.mult, [stg, pv], [win])
            TS(e, win[:, c, C_KR2 + 80:C_KR2 + 96], stg[:, 1088:1104], g, None, ALU.mult, None, [stg, pv], [win])
            stg = stg_pool.get()
            S.dma("sp", stg[:, 0:1152], W["w_in"][l, c * 128:(c + 1) * 128, 1120:DIN], reads=[W["w_in"]], writes=[stg])
            TS(e, win[:, c, C_GB:WCOLS], stg[:, 0:1152], g, None, ALU.mult, None, [stg, pv], [win])
        for c in range(8):
            stg = stg_pool.get()
            S.dma("sp", stg[:, 0:D], W["w_out"][l, c * 128:(c + 1) * 128, :], reads=[W["w_out"]], writes=[stg])
            TS("dve", wout[:, c, :], stg[:, 0:D], pv[:, PV_BG + c:PV_BG + c + 1], None, ALU.mult, None, [stg, pv], [wout])
        stg = stg_pool.get()
        S.dma("sp", stg[0:96, 0:1152].rearrange("p (c n) -> p c n", c=2),
              W["w_uq"][l, :, :].rearrange("(c p) n -> p c n", p=96), reads=[W["w_uq"]], writes=[stg])
        for c in range(2):
            g = pv[0:96, PV_QG + c:PV_QG + c + 1]
            src = stg[0:96, c * 576:(c + 1) * 576]
            TS("dve", wuq[:, c, :], src, g, None, ALU.mult, None, [stg, pv], [wuq])
            s3 = src.rearrange("p (h d) -> p h d", d=96)
            d3 = wuqr[:, c, :].rearrange("p (h d) -> p h d", d=96)
            TS("dve", d3[:, :, 64:80], s3[:, :, 80:96], g, -1.0, ALU.mult, ALU.mult, [stg, pv], [wuqr])
            TS("dve", d3[:, :, 80:96], s3[:, :, 64:80], g, None, ALU.mult, None, [stg, pv], [wuqr])
        stg = stg_pool.get()
        S.dma("sp", stg[:, 0:768], W["w_ukv"][l, :, :], reads=[W["w_ukv"]], writes=[stg])
        g = pv[:, PV_KVG:PV_KVG + 1]
        TS("dve", wukv[:, 0:768], stg[:, 0:768], g, None, ALU.mult, None, [stg, pv], [wukv])
        TS("dve", wv[:, :].rearrange("p (h d) -> p h d", d=64),
           stg[:, 0:768].rearrange("p (h d) -> p h d", d=128)[:, :, 64:128], g, None, ALU.mult, None, [stg, pv], [wv])
        stg = stg_pool.get()
        MSET("dve", stg[:, 0:768], 0.0, [stg])
        for wi, nm in enumerate(["lru_wa", "lru_wx"]):
            for c in range(3):
                for hh in range(2):
                    base = wi * 384 + c * 128
                    S.dma("sp", stg[hh * 64:(hh + 1) * 64, base + hh * 64:base + hh * 64 + 64], W[nm][l, 2 * c + hh, :, :],
                          reads=[W[nm]], writes=[stg])
        CP("dve", wabd[:, :, :], stg[:, 0:384].rearrange("p (c n) -> p c n", c=3), [stg], [wabd])
        CP("dve", wxbd[:, :, :], stg[:, 384:768].rearrange("p (c n) -> p c n", c=3), [stg], [wxbd])
        stg = stg_pool.get()
        S.dma("sp", stg[:, 0:512].rearrange("p (g j) -> p g j", g=4),
              W["sgu_w"][l, :, :, :].rearrange("g i j -> i g j"), reads=[W["sgu_w"]], writes=[stg])
        MSET("dve", stg[0:64, 0:512].rearrange("p (g j) -> p g j", g=4)[:, :, 64:128], 0.0, [stg])
        CP("dve", wstmp[:, :, :], stg[:, 0:512].rearrange("p (g j) -> p g j", g=4), [stg], [wstmp])
        for g4 in range(4):
            S.op("pe", lambda: npe.transpose(ps_t[:, g4 * 128:(g4 + 1) * 128], wstmp[:, g4, :], ident[:]), [wstmp, ident], [ps_t], attach=False)
        CP("dve", wsT[:, :, :], ps_t[:, 0:512].rearrange("p (g i) -> p g i", g=4), [ps_t], [wsT])

    def branch_rms(ch0, nch, nfeat):
        pss = PG.get()
        for i in range(nch):
            sq = Tb.get()
            TT("pool", sq[:], yf[:, ch0 + i, :], yf[:, ch0 + i, :], ALU.mult, [yf], [sq])
            MM(pss[:, 0:T], ones[:], sq[:], i == 0, i == nch - 1, [ones, sq], [pss])
        rs = Tf.get()
        rstd_from(pss[:, 0:T], nfeat, rs[:], [pss], [rs])
        for i in range(nch):
            TT("dve", yT[:, ch0 + i, :], yf[:, ch0 + i, :], rs[:], ALU.mult, [yf, rs], [yT])

    state = {"hs_prev": None}

    def p0_start(j, src_ap_fn, src_reg):
        xt = xt_pool.get()
        S.dma("sp", xt[:, :].rearrange("p (b d) -> p b d", b=NB),
              src_ap_fn(j).rearrange("(b p) d -> p b d", p=128), reads=[src_reg], writes=[xt])
        st = st_pool.get()

        def tabs():
            S.dma("sp", ctab[:], tab_d[0, :, j * T:(j + 1) * T], reads=[tabreg[j]], writes=[ctab])
            S.dma("sp", stab[:], tab_d[1, :, j * T:(j + 1) * T], reads=[tabreg[j]], writes=[stab])

        def g_p0(b):
            if b == 0:
                tabs()
            junk = hb_pool.get()
            ACT(junk[:], xt[:, b * D:(b + 1) * D], AF.Square, [xt], [junk, st], accum_out=st[:, b:b + 1])
            yield
            ACT(st[:, 8 + b:9 + b], st[:, b:b + 1], AF.Ln, [st], [st], scale=1.0 / D, bias=EPS)
            yield
            ACT(st[:, 8 + b:9 + b], st[:, 8 + b:9 + b], AF.Exp, [st], [st], scale=-0.5)
            yield
            hb = hb_pool.get()
            ACT(hb[:], xt[:, b * D:(b + 1) * D], AF.Copy, [xt, st], [hb], scale=st[:, 8 + b:9 + b])
            yield
            for c in range(8):
                S.op("pe", lambda: npe.transpose(ps_t[:, c * 128:(c + 1) * 128], hb[:, c * 128:(c + 1) * 128], ident[:]),
                     [hb, ident], [ps_t], attach=False)
            CP("act" if b % 2 == 0 else "dve", hT[:, :, b * 128:(b + 1) * 128],
               ps_t[:, :].rearrange("p (c n) -> p c n", c=8), [ps_t], [hT])
            yield

        return xt, [g_p0(b) for b in range(NB)]

    def tile_layer(l, j, src_ap_fn, src_reg, dst_ap_fn, dst_reg, kvwrite, pre=None, nxt=None):
        if pre is None:
            xt, gens0 = p0_start(j, src_ap_fn, src_reg)
            run_rr(gens0, NB, [hb_pool])
        else:
            xt = pre
        result = {"next_xt": None}

        ps_t_f = "pst"

        def proj(col0, m, bank=None):
            if bank == "pst":
                v = ps_t[:, :].bitcast(F32)
                for c in range(8):
                    MM(v[0:m, 0:T], win[:, c, col0:col0 + m], hT[:, c, :], c == 0, c == 7, [win, hT], [ps_t])
                return ps_t
            ps = bank if bank is not None else PG.get()
            for c in range(8):
                MM(ps[0:m, 0:T], win[:, c, col0:col0 + m], hT[:, c, :], c == 0, c == 7, [win, hT], [ps])
            return ps

        pst_f = ps_t[:, :].bitcast(F32)

        def lat_gen():
            pss = ps_o[2]
            sq0, sq1, sqk = Tb.get(), Tb.get(), Tb.get()
            rs, kvl = Tf.get(), Tf.get()
            proj(C_QL, 96, bank=ps_t_f)
            yield
            CP("act", ql[:, 0, :], pst_f[0:96, 0:T], [ps_t], [ql])
            yield
            proj(C_QL + 96, 96, bank=ps_t_f)
            TT("pool", sq0[0:96, :], ql[:, 0, :], ql[:, 0, :], ALU.mult, [ql], [sq0])
            yield
            CP("act", ql[:, 1, :], pst_f[0:96, 0:T], [ps_t], [ql])
            MM(pss[0:96, 0:T], ones[0:96, 0:96], sq0[0:96, :], True, False, [ones, sq0], [pss])
            yield
            TT("pool", sq1[0:96, :], ql[:, 1, :], ql[:, 1, :], ALU.mult, [ql], [sq1])
            proj(C_KV, 128, bank=ps_t_f)
            yield
            MM(pss[0:96, 0:T], ones[0:96, 0:96], sq1[0:96, :], False, True, [ones, sq1], [pss])
            CP("act", kvl[:], pst_f[:, 0:T], [ps_t], [kvl])
            yield
            ACT(rs[0:96, :], pss[0:96, 0:T], AF.Ln, [pss], [rs], scale=1.0 / 192.0, bias=EPS)
            TT("pool", sqk[:], kvl[:], kvl[:], ALU.mult, [kvl], [sqk])
            yield
            ACT(rs[0:96, :], rs[0:96, :], AF.Exp, [rs], [rs], scale=-0.5)
            yield
            for c in range(2):
                TT("dve", qln[:, c, :], ql[:, c, :], rs[0:96, :], ALU.mult, [ql, rs], [qln])
            MM(pss[:, 0:T], ones[:], sqk[:], True, True, [ones, sqk], [pss])
            yield
            ACT(rs[:], pss[:, 0:T], AF.Ln, [pss, qln], [rs], scale=1.0 / 128.0, bias=EPS)
            yield
            ACT(rs[:], rs[:], AF.Exp, [rs], [rs], scale=-0.5)
            yield
            TT("dve", ckvT[:], kvl[:], rs[:], ALU.mult, [kvl, rs], [ckvT])
            yield

        hs = hs_pool.get()
        hs_prev = state["hs_prev"]
        LB = [ps_g[0], ps_g[1], ps_s[0], ps_s[1], ps_o[0], ps_o[1]]

        def lru_gen(c):
            cv, ga_, gx_, a_, m_ = Tf.get(), Tf.get(), Tf.get(), Tf.get(), Tf.get()
            cvb = Tb.get()
            psa, psx = LB[2 * c], LB[2 * c + 1]
            ps = proj(C_XA + c * 128, 128, bank=psa)
            CP("act", xa_sb3[c][:, 3:T + 3], ps[:, 0:T], [ps], [xa_sb3[c]])
            yield
            cw = lambda k: pv[:, PV_CW + 3 * k + c:PV_CW + 3 * k + c + 1]
            xc = xa_sb3[c]
            TS("dve", cv[:], xc[:, 0:T], cw(0), pv[:, PV_CB + c:PV_CB + c + 1], ALU.mult, ALU.add, [xc, pv], [cv])
            for k in range(1, 4):
                STT(cv[:], xc[:, k:k + T], cw(k), cv[:], ALU.mult, ALU.add, [xc, pv, cv], [cv])
            yield
            CP("pool", xc[:, 0:3], xc[:, T:T + 3], [xc], [xc])
            CP("act", cvb[:], cv[:], [cv], [cvb])
            yield
            MM(psa[:, 0:T], wabd[:, c, :], cvb[:], True, True, [wabd, cvb], [psa])
            MM(psx[:, 0:T], wxbd[:, c, :], cvb[:], True, True, [wxbd, cvb], [psx])
            yield
            ACT(ga_[:], psa[:, 0:T], AF.Exp, [psa, pv], [ga_], scale=-1.0, bias=pv[:, PV_NBA + c:PV_NBA + c + 1])
            ACT(gx_[:], psx[:, 0:T], AF.Exp, [psx, pv], [gx_], scale=-1.0, bias=pv[:, PV_NBX + c:PV_NBX + c + 1])
            yield
            ACT(ga_[:], ga_[:], AF.Ln, [ga_], [ga_], bias=1.0)
            ACT(gx_[:], gx_[:], AF.Ln, [gx_], [gx_], bias=1.0)
            yield
            ACT(ga_[:], ga_[:], AF.Exp, [ga_], [ga_], scale=-1.0)
            ACT(gx_[:], gx_[:], AF.Exp, [gx_], [gx_], scale=-1.0)
            yield
            ACT(a_[:], ga_[:], AF.Exp, [ga_, pv], [a_], scale=pv[:, PV_CC + c:PV_CC + c + 1])
            ACT(m_[:], ga_[:], AF.Exp, [ga_, pv], [m_], scale=pv[:, PV_CC + 3 + c:PV_CC + 4 + c])
            yield
            TS("dve", m_[:], m_[:], -1.0, 1.0, ALU.mult, ALU.add, [m_], [m_])
            yield
            ACT(m_[:], m_[:], AF.Ln, [m_], [m_])
            TT("dve", gx_[:], gx_[:], cv[:], ALU.mult, [gx_, cv], [gx_])
            yield
            ACT(m_[:], m_[:], AF.Exp, [m_], [m_], scale=0.5)
            yield
            TT("dve", gx_[:], gx_[:], m_[:], ALU.mult, [gx_, m_], [gx_])
            init = 0.0 if hs_prev is None else hs_prev[:, c, T - 1:T]
            S.op("dve", lambda: nv.tensor_tensor_scan(out=hs[:, c, :], data0=a_[:], data1=gx_[:], initial=init,
                                                      op0=ALU.mult, op1=ALU.add),
                 [a_, gx_] + ([hs_prev] if hs_prev is not None else []), [hs])
            yield

        run_rr([lat_gen()] + [lru_gen(c) for c in range(3)], 4, [Tf, Tb], stagger=True)
        state["hs_prev"] = hs

        def silu2(ps_ap, psbuf, dst_ap, dstbuf):
            t_ = Tf.get()
            ACT(t_[:], ps_ap, AF.Tanh, [psbuf], [t_], scale=0.5)
            STT(dst_ap, t_[:], 1.0, ps_ap, ALU.add, ALU.mult, [t_, psbuf], [dstbuf])

        def gelu2(ps_ap, psbuf, dst_ap, dstbuf, w):
            t_ = Tf.get()
            ACT(t_[:, 0:w], ps_ap, AF.Square, [psbuf], [t_])
            TS("dve", t_[:, 0:w], t_[:, 0:w], 0.044715, 1.0, ALU.mult, ALU.add, [t_], [t_])
            TT("dve", t_[:, 0:w], t_[:, 0:w], ps_ap, ALU.mult, [t_, psbuf], [t_])
            ACT(t_[:, 0:w], t_[:, 0:w], AF.Tanh, [t_], [t_], scale=0.7978845608028654)
            STT(dst_ap, t_[:, 0:w], 1.0, ps_ap, ALU.add, ALU.mult, [t_, psbuf], [dstbuf])

        BB = Rot([ps_g[0], ps_g[1], ps_s[0], ps_s[1], ps_o[0], ps_o[1], ps_o[2]])

        def grp_proj(col0, m, bk):
            for c in range(8):
                MM(bk[0:m, 0:T], win[:, c, col0:col0 + m], hT[:, c, :], c == 0, c == 7, [win, hT], [bk])

        def tanh_half(bk, w):
            t_ = Tf.get()
            ACT(t_[:, 0:w], bk[:, 0:w], AF.Tanh, [bk], [t_], scale=0.5)
            return t_

        def gelu_stages(bk, w, dst_ap, dstbuf):
            t_ = Tf.get()
            ACT(t_[:, 0:w], bk[:, 0:w], AF.Square, [bk], [t_])
            yield
            TS("dve", t_[:, 0:w], t_[:, 0:w], 0.044715, 1.0, ALU.mult, ALU.add, [t_], [t_])
            TT("dve", t_[:, 0:w], t_[:, 0:w], bk[:, 0:w], ALU.mult, [t_, bk], [t_])
            yield
            ACT(t_[:, 0:w], t_[:, 0:w], AF.Tanh, [t_], [t_], scale=0.7978845608028654)
            yield
            STT(dst_ap, t_[:, 0:w], 1.0, bk[:, 0:w], ALU.add, ALU.mult, [t_, bk], [dstbuf])
            yield

        def g_ga(c):
            bk = BB.get()
            grp_proj(C_GA + c * 128, 128, bk)
            yield
            t_ = tanh_half(bk, T)
            yield
            STT(t_[:], t_[:], 1.0, bk[:, 0:T], ALU.add, ALU.mult, [t_, bk], [t_])
            STT(yf[:, c, :], t_[:], 0.5, hs[:, c, :], ALU.mult, ALU.mult, [t_, hs], [yfA])
            yield

        def g_ugc(cc):
            bk = BB.get()
            grp_proj(C_U + cc * 128, 128, bk)
            yield
            gu = Tf.get()
            yield from gelu_stages(bk, T, gu[:], gu)
            bk2 = BB.get()
            grp_proj(C_GC + cc * 128, 128, bk2)
            yield
            t2 = tanh_half(bk2, T)
            yield
            STT(t2[:], t2[:], 1.0, bk2[:, 0:T], ALU.add, ALU.mult, [t2, bk2], [t2])
            STT(ugc[:, cc, :], gu[:], 0.25, t2[:], ALU.mult, ALU.mult, [gu, t2], [ugc])
            yield

        def g_v(b):
            bk = BB.get()
            for c in range(8):
                MM(bk[:, 0:256], hT[:, c, b * 128:(b + 1) * 128], win[:, c, C_V:C_V + 256], c == 0, c == 7, [hT, win], [bk])
            yield
            yield from gelu_stages(bk, 256, gvb[:, b, :], gvb)

        def g_gb(c):
            bk = BB.get()
            grp_proj(C_GB + c * 128, 128, bk)
            yield
            t_ = tanh_half(bk, T)
            yield
            STT(sgb[:, c, :], t_[:], 1.0, bk[:, 0:T], ALU.add, ALU.mult, [t_, bk], [sgb])
            yield

        def g_q(h):
            pA, pB = BB.get(), BB.get()
            for c in range(2):
                MM(pA[0:96, 0:T], wuq[:, c, h * 96:(h + 1) * 96], qln[:, c, :], c == 0, c == 1, [wuq, qln], [pA])
            for c in range(2):
                MM(pB[0:96, 0:T], wuqr[:, c, h * 96:(h + 1) * 96], qln[:, c, :], c == 0, c == 1, [wuqr, qln], [pB])
            yield
            t1, t2 = Tf.get(), Tf.get()
            TT("dve", t1[0:96, :], pA[0:96, 0:T], ctab[:], ALU.mult, [pA, ctab], [t1])
            TT("dve", t2[0:96, :], pB[0:96, 0:T], stab[:], ALU.mult, [pB, stab], [t2])
            yield
            TT("pool", qT[:, h, :], t1[0:96, :], t2[0:96, :], ALU.add, [t1, t2], [qT_h[h]])
            yield

        def g_kpe():
            p1, p2 = BB.get(), BB.get()
            grp_proj(C_KR1, 96, p1)
            grp_proj(C_KR2, 96, p2)
            yield
            t1, t2 = Tf.get(), Tf.get()
            TT("dve", t1[64:96, :], p1[64:96, 0:T], ctab[64:96, :], ALU.mult, [p1, ctab], [t1])
            TT("dve", t2[64:96, :], p2[64:96, 0:T], stab[64:96, :], ALU.mult, [p2, stab], [t2])
            yield
            TT("pool", kpe[64:96, :], t1[64:96, :], t2[64:96, :], ALU.add, [t1, t2], [kpe])
            yield

        def g_k(h):
            ps = BB.get()
            MM(ps[0:96, 0:T], wukv[:, h * 128:h * 128 + 96], ckvT[:], True, True, [wukv, ckvT], [ps])
            yield
            CP("act", kcur[0:64, h, :], ps[0:64, 0:T], [ps], [kcur_h[h]])
            CP("pool", kcur[64:96, h, :], kpe[64:96, :], [kpe], [kcur_h[h]])
            yield

        def g_vp(b):
            ps = BB.get()
            MM(ps[:, 0:384], ckvT[:, b * 128:(b + 1) * 128], wv[:], True, True, [ckvT, wv], [ps])
            yield
            CP("act", vcur[:, b, :, 0:64], ps[:, 0:384].rearrange("p (h d) -> p h d", d=64), [ps], [vcur_b[b]])
            yield

        qk = []
        for h in range(NH):
            qk += [g_q(h), g_k(h)]
        run_rr([g_ga(c) for c in range(3)] + [g_ugc(cc) for cc in range(2)] + [g_v(b) for b in range(NB)]
               + [g_kpe()] + [g_gb(c) for c in range(3)] + qk + [g_vp(b) for b in range(NB)], 4, [Tf, BB], stagger=True)
        if kvwrite:
            kr = Buf("k_%d_%d" % (l, j))
            vr = Buf("v_%d_%d" % (l, j))
            kreg[(l, j)] = kr
            vreg[(l, j)] = vr
            cj, o_ = (j * T) // KCH, (j * T) % KCH
            for hg in range(2):
                S.dma("pool", kc_d[l, hg, :, cj, :, o_:o_ + T], kcur[:, hg * 3:(hg + 1) * 3, :],
                      reads=kcur_h[hg * 3:(hg + 1) * 3], writes=[kr], sembuf=kcur)
                S.dma("pool", vc_d[l, hg, :, j * NB:(j + 1) * NB, :],
                      vcur[:, :, hg * 3:(hg + 1) * 3, :].rearrange("p b h d -> p b (h d)"),
                      reads=vcur_b, writes=[vr], sembuf=vcur)

        npre = j * NB
        scale = float(DQK ** -0.5)
        PER = 512 // T
        LAG = 3
        for hg in range(2):
            po = ps_o
            first = [True, True, True]
            loaded = {}

            def chunk(ci):
                if ci not in loaded:
                    k0 = ci * KCH
                    nk = min(KCH, npre * 128 - k0)
                    nkb = nk // 128
                    kb_ = kb_pool.get()
                    vb_ = vb_pool.get()
                    tiles_needed = sorted(set((k0 + i * 128) // T for i in range(nkb)))
                    S.dma("sp", kb_[:, :, 0:nk], kc_d[l, hg, :, ci, :, 0:nk],
                          reads=[kreg[(l, tj)] for tj in tiles_needed], writes=[kb_])
                    S.dma("sp", vb_[:, 0:nkb, :], vc_d[l, hg, :, k0 // 128:k0 // 128 + nkb, :],
                          reads=[vreg[(l, tj)] for tj in tiles_needed], writes=[vb_])
                    loaded[ci] = (kb_, vb_)
                return loaded[ci]

            items = [("p", kb) for kb in range(npre)] + [("d", b) for b in range(NB)]
            groups = [items[i:i + PER] for i in range(0, len(items), PER)]
            units = [(g, hh) for g in groups for hh in range(3)]
            pend = []

            def emit_S2(su, pair):
                psw, bA, bB = PW[su % 2]
                pt = Pp.get()
                outs = []
                allfull = True
                for ui, (g, hh) in enumerate(pair):
                    h = hg * 3 + hh
                    bank = (bA, bB)[ui]
                    off = ui * 512
                    info = []
                    for slot, (kind, idx) in enumerate(g):
                        c0 = slot * T
                        if kind == "p":
                            kb_, vb_ = chunk(idx * 128 // KCH)
                            kk = idx - (idx * 128 // KCH) * (KCH // 128)
                            MM(bank[:, c0:c0 + T], kb_[:, hh, kk * 128:(kk + 1) * 128], qT[:, h, :], True, True, [kb_, qT_h[h]], [bank])
                            info.append((off + c0, 0, vb_[:, kk, hh * 65:(hh + 1) * 65], vb_, False))
                        else:
                            q0 = idx * 128
                            MM(bank[:, c0 + q0:c0 + T], kcur[:, h, idx * 128:(idx + 1) * 128], qT[:, h, q0:T], True, True,
                               [kcur_h[h], qT_h[h]], [bank])
                            info.append((off + c0, q0, vcur[:, idx, h, :], vcur_b[idx], True))
                    if not (len(info) == PER and all(q0 == 0 for (_, q0, _, _, _) in info)):
                        allfull = False
                    outs.append((hh, pt, info, bank, off))
                if allfull and len(pair) == 2:
                    ACT(pt[:, 0:1024], psw[:, 0:1024], AF.Exp, [bA, bB], [pt], scale=scale)
                else:
                    for (hh, _, info, bank, off) in outs:
                        if all(q0 == 0 for (_, q0, _, _, _) in info):
                            w = len(info) * T
                            ACT(pt[:, off:off + w], bank[:, 0:w], AF.Exp, [bank], [pt], scale=scale)
                        else:
                            for (c0, q0, _, _, _) in info:
                                ACT(pt[:, c0 + q0:c0 + T], bank[:, c0 - off + q0:c0 - off + T], AF.Exp, [bank], [pt], scale=scale)
                for (hh, _, info, bank, off) in outs:
                    for (c0, q0, _, _, isd) in info:
                        if isd:
                            MSET("pool", pt[64:128, c0 + q0:c0 + q0 + 64], 0.0, [pt])
                return [(hh, pt, info) for (hh, _, info, _, _) in outs]

            def emit_PV(u, last):
                hh, pt, info = u
                for n_, (c0, q0, vap, vbuf_, isd) in enumerate(info):
                    MM(po[hh][0:65, q0:T], vap, pt[:, c0 + q0:c0 + T], first[hh], last and n_ == len(info) - 1, [vbuf_, pt], [po[hh]])
                    first[hh] = False

            pairs = [units[i:i + 2] for i in range(0, len(units), 2)]
            ucount = 0
            for i in range(len(pairs) + 1):
                if i < len(pairs):
                    pend.append(emit_S2(i, pairs[i]))
                if i >= 1:
                    for u in pend[i - 1]:
                        emit_PV(u, ucount >= len(units) - 3)
                        ucount += 1
            def g_norm(hh):
                h = hg * 3 + hh
                zhi, zlo = zhi3[hh], zlo3[hh]
                TS("dve", zhi[64:65, :], po[hh][64:65, 0:T], 1.0, None, ALU.mult, None, [po[hh]], [zhi])
                TT("dve", zlo[64:65, :], po[hh][64:65, 0:T], zhi[64:65, :], ALU.subtract, [po[hh], zhi], [zlo])
                yield
                pb = PG.get()
                MM(pb[:, 0:T], esel[:], zhi[:], True, False, [esel, zhi], [pb])
                MM(pb[:, 0:T], esel[:], zlo[:], False, True, [esel, zlo], [pb])
                yield
                rc = Tf.get()
                ACT(rc[0:64, :], pb[0:64, 0:T], AF.Ln, [pb], [rc])
                yield
                ACT(rc[0:64, :], rc[0:64, :], AF.Exp, [rc], [rc], scale=-1.0)
                yield
                ch, off = h // 2, (h % 2) * 64
                po_sb = Tf.get()
                TT("dve", po_sb[0:64, :], po[hh][0:64, 0:T], rc[0:64, :], ALU.mult, [po[hh], rc], [po_sb])
                yield
                if off:
                    sh = Tf.get()
                    CP("act", sh[off:off + 64, :], po_sb[0:64, :], [po_sb], [sh])
                    yield
                else:
                    sh = po_sb
                STT(yf[off:off + 64, 3 + ch, :], sh[off:off + 64, :], 0.5, sgb[off:off + 64, ch, :], ALU.mult, ALU.mult, [sh, sgb], [yfB])
                yield

            if hg == 0:
                run_rr([g_norm(hh) for hh in range(3)], 3, [Tf, PG])
            else:
                norm_gens = [g_norm(hh) for hh in range(3)]
        def g_rms(ch0, nch, nfeat):
            yfX, yTX = (yfA, yT) if ch0 == 0 else (yfB, yT_b)
            pss = PG.get()
            for i in range(nch):
                sq = Tb.get()
                TT("pool", sq[:], yf[:, ch0 + i, :], yf[:, ch0 + i, :], ALU.mult, [yfX], [sq])
                MM(pss[:, 0:T], ones[:], sq[:], i == 0, i == nch - 1, [ones, sq], [pss])
            yield
            rs = Tf.get()
            ACT(rs[:], pss[:, 0:T], AF.Ln, [pss], [rs], scale=1.0 / nfeat, bias=EPS)
            yield
            ACT(rs[:], rs[:], AF.Exp, [rs], [rs], scale=-0.5)
            yield
            for i in range(nch):
                TT("pool" if i == 0 else "dve", yT[:, ch0 + i, :], yf[:, ch0 + i, :], rs[:], ALU.mult, [yfX, rs], [yTX])
            yield

        psm = [ps_o[0], ps_o[1]]

        def g_sgu(b):
            gv = gvb[:, b, :]
            vnA, vnB = vnA2[b], vnB2[b]
            st2 = st_pool.get()
            S.op("dve", lambda: nv.bn_stats(out=st2[:, 0:6], in_=gv), [gvb], [st2])
            S.op("dve", lambda: nv.bn_aggr(out=st2[:, 8:10], in_=st2[:, 0:6]), [st2], [st2])
            yield
            ACT(st2[:, 10:11], st2[:, 9:10], AF.Ln, [st2], [st2], scale=0.25, bias=EPS)
            yield
            ACT(st2[:, 10:11], st2[:, 10:11], AF.Exp, [st2], [st2], scale=-0.5)
            yield
            vn_ = Tf.get()
            TS("dve", vn_[:, 0:256], gv, st2[:, 8:9], st2[:, 10:11], ALU.subtract, ALU.mult, [gvb, st2], [vn_])
            STT(vn_[:, 0:256], vn_[:, 0:256], 0.5, sgn_g[:], ALU.mult, ALU.mult, [vn_, sgn_g], [vn_])
            yield
            g3 = vn_[:, 0:256].rearrange("p (g c) -> p g c", c=64)
            b3 = sgn_b[:, :].rearrange("p (g c) -> p g c", c=64)
            for par, vn in ((0, vnA), (1, vnB)):
                v3 = vn[:, :].rearrange("p (g c) -> p g c", c=64)
                for cc in range(2):
                    gi = 2 * cc + par
                    TT("pool", v3[:, gi, :], g3[:, gi, :], b3[:, gi, :], ALU.add, [vn_, sgn_b], [vn])
            yield
            for cc in range(2):
                MM(psm[cc][:, b * 128:(b + 1) * 128], vnA[:, cc * 128:(cc + 1) * 128], wsT[:, 2 * cc, :], True, False,
                   [vnA, wsT], [psm[cc]])
                MM(psm[cc][:, b * 128:(b + 1) * 128], vnB[:, cc * 128:(cc + 1) * 128], wsT[:, 2 * cc + 1, :], False, True,
                   [vnB, wsT], [psm[cc]])
            yield

        extra = []
        if nxt is not None:
            result["next_xt"], extra = nxt()
        run_rr(norm_gens + [g_sgu(b) for b in range(NB)] + [g_rms(0, 3, 384)] + extra, 8, [Tf, PG, st_pool, hb_pool, Tb], stagger=True)
        def g_rmsC():
            tms = [Tf.get(), Tf.get()]
            for cc in range(2):
                TT("dve", tms[cc][:], psm[cc][:, 0:T], bsbc[:, cc, :], ALU.add, [psm[cc], bsbc], [tms[cc]])
            yield
            for cc in range(2):
                TT("dve", yf[:, 6 + cc, :], tms[cc][:], ugc[:, cc, :], ALU.mult, [tms[cc], ugc], [yfC])
            yield
            pss = ps_s[1]
            for i in range(2):
                sq = Tb.get()
                TT("pool", sq[:], yf[:, 6 + i, :], yf[:, 6 + i, :], ALU.mult, [yfC], [sq])
                MM(pss[:, 0:T], ones[:], sq[:], i == 0, i == 1, [ones, sq], [pss])
            yield
            rs = Tf.get()
            ACT(rs[:], pss[:, 0:T], AF.Ln, [pss], [rs], scale=1.0 / 256.0, bias=EPS)
            yield
            ACT(rs[:], rs[:], AF.Exp, [rs], [rs], scale=-0.5)
            yield
            for i in range(2):
                TT("dve", yT_c[i][:, 6 + i, :], yf[:, 6 + i, :], rs[:], ALU.mult, [yfC, rs], [yT_c[i]])
            yield

        OB = [ps_o[2], ps_g[0], ps_g[1], ps_s[0]]
        assert NB <= 2

        def g_out(b):
            pso = [OB[2 * b], OB[2 * b + 1]]
            st3 = st_pool.get()
            for half in range(2):
                for c in range(3):
                    MM(pso[half][:, :], yT[:, c, b * 128:(b + 1) * 128], wout[:, c, half * 512:(half + 1) * 512], c == 0, False,
                       [yT, wout], [pso[half]])
            yield
            yield
            for half in range(2):
                for c in range(3, 6):
                    MM(pso[half][:, :], yT[:, c, b * 128:(b + 1) * 128], wout[:, c, half * 512:(half + 1) * 512], False, False,
                       [yT_b, wout], [pso[half]])
            yield
            yield
            yield
            for half in range(2):
                for c in range(6, 8):
                    MM(pso[half][:, :], yT[:, c, b * 128:(b + 1) * 128], wout[:, c, half * 512:(half + 1) * 512], False, c == 7,
                       [yT_c[c - 6], wout], [pso[half]])
            yield
            for half in range(2):
                junk = hb_pool.get()
                ACT(junk[:, 0:512], pso[half][:, :], AF.Square, [pso[half]], [junk, st3], accum_out=st3[:, half:half + 1])
            yield
            TT("dve", st3[:, 2:3], st3[:, 0:1], st3[:, 1:2], ALU.add, [st3], [st3])
            yield
            ACT(st3[:, 3:4], st3[:, 2:3], AF.Ln, [st3], [st3], scale=1.0 / D, bias=EPS)
            yield
            ACT(st3[:, 3:4], st3[:, 3:4], AF.Exp, [st3], [st3], scale=-0.5)
            yield
            for half in range(2):
                xs_ = xt[:, b * D + half * 512:b * D + (half + 1) * 512]
                o1 = o1_pool.get()
                STT(o1[:], pso[half][:, :], st3[:, 3:4], gpost[:, half * 512:(half + 1) * 512], ALU.mult, ALU.mult,
                    [pso[half], st3, gpost], [o1])
                TT("dve", xs_, xs_, o1[:], ALU.add, [xt, o1], [xt])
                yield

        run_rr([g_rms(3, 3, 384), g_rmsC()] + [g_out(b) for b in range(NB)], NB + 2, [Tf, Tb, st_pool, PG], stagger=True)
        S.dma("pool", dst_ap_fn(j).rearrange("(b p) d -> p b d", p=128), xt[:, :].rearrange("p (b d) -> p b d", b=NB),
              reads=[xt], writes=[dst_reg], sembuf=xt)
        return result["next_xt"]

    o1_pool = Rot([S.sbuf("o1_%d" % i, [128, 512], F32) for i in range(3)])

    OVERLAP_P0 = True
    outregs = []
    for l in range(NL):
        prep_layer(l)
        state["hs_prev"] = None
        for c3 in range(3):
            MSET("dve", xa_sb3[c3][:], 0.0, [xa_sb3[c3]])
        src_t = x_in if l == 0 else xs_d[(l - 1) % 2]
        dst_t = out_d if l == NL - 1 else xs_d[l % 2]
        pre = None
        srcf = (lambda jj, st=src_t: st[jj * T:(jj + 1) * T, :])
        for j in range(NT):
            sreg = x_in if l == 0 else xr((l - 1, j))
            dreg = xr((l, j))
            if l == NL - 1:
                outregs.append(dreg)
            nxt = None
            if j + 1 < NT and OVERLAP_P0:
                sreg_n = x_in if l == 0 else xr((l - 1, j + 1))
                nxt = (lambda jn=j + 1, sr=sreg_n: p0_start(jn, srcf, sr))
            pre = tile_layer(l, j, srcf, sreg,
                             (lambda jj, dt=dst_t: dt[jj * T:(jj + 1) * T, :]), dreg, kvwrite=(j < NT - 1), pre=pre, nxt=nxt)
    S.finish(outregs, "sp")
    S.finish(outregs, "pool")
    return nc, S


def inv_freq_table():
    half = 16
    f = (10000.0 ** (-np.arange(half, dtype=np.float32) / half)).astype(np.float32)
    t = np.zeros((96, 1), np.float32)
    t[64:80, 0] = f
    t[80:96, 0] = f
    return t


WNAMES = ["pre_norm_g", "w_in", "conv_w", "conv_b", "lru_wa", "lru_ba", "lru_wx", "lru_bx", "lru_lambda", "q_norm_g",
          "w_uq", "kv_norm_g", "w_ukv", "sgu_norm_g", "sgu_norm_b", "sgu_w", "sgu_b", "branch_norm_g", "w_out",
          "post_norm_g"]

T_TILE = 256


def kernel(**inputs):
    x = np.ascontiguousarray(np.asarray(inputs["x"], dtype=np.float32))
    pos = np.ascontiguousarray(np.asarray(inputs["positions"], dtype=np.int32))
    B, SEQ, _ = x.shape
    NT = SEQ // T_TILE
    nc, S = build(NT, 4, T_TILE)
    wmap = {nm: np.ascontiguousarray(np.asarray(inputs[nm], dtype=np.float32)) for nm in WNAMES}
    invf = inv_freq_table()
    in_maps = []
    NCORES = B
    for core in range(NCORES):
        b = core % B
        m = {"x": x[b], "positions": pos[b].reshape(1, SEQ), "invf": invf}
        m.update(wmap)
        in_maps.append(m)
    res = run_bass_kernel_spmd(nc, in_maps, core_ids=list(range(NCORES)))
    out = np.stack([np.asarray(res.results[b]["out"], dtype=np.float32) for b in range(B)], axis=0)
    return out
```

```python
import numpy as np
import concourse.bass as bass
import concourse.mybir as mybir
from concourse.bass_utils import run_bass_kernel_spmd

F32 = mybir.dt.float32
BF16 = mybir.dt.bfloat16
I32 = mybir.dt.int32
AF = mybir.ActivationFunctionType
ALU = mybir.AluOpType
AX = mybir.AxisListType

D = 1024
DIN = 2272
LW = 384
NH = 6
DQK = 96
EPS = 1e-6
WCOLS = 2432
C_XA, C_GA, C_QL, C_KV, C_KR1, C_KR2, C_GB, C_U, C_V, C_GC = 0, 384, 768, 960, 1088, 1184, 1280, 1664, 1920, 2176


ATTACH_WAITS = True


class Buf:
    __slots__ = ("name", "t", "last_w", "readers", "sem", "semcnt")

    def __init__(self, name, t=None):
        self.name = name
        self.t = t
        self.last_w = None
        self.readers = []
        self.sem = None
        self.semcnt = 0

    def __getitem__(self, idx):
        return self.t[idx]


class Alias:
    def __init__(self, base, t):
        object.__setattr__(self, "base", base)
        object.__setattr__(self, "t", t)

    def __getitem__(self, idx):
        return self.t[idx]

    def __getattr__(self, k):
        return getattr(self.base, k)

    def __setattr__(self, k, v):
        setattr(self.base, k, v)


class Sched:
    def __init__(self, nc):
        self.nc = nc
        self.eng = {"pe": nc.tensor, "act": nc.scalar, "dve": nc.vector,
                    "pool": nc.gpsimd, "sp": nc.sync}
        self.sem = {k: nc.alloc_semaphore("prog_" + k) for k in self.eng}
        self.cnt = {k: 0 for k in self.eng}
        self.waited = {k: {} for k in self.eng}
        self.nwaits = 0
        self.nins = 0
        self.pe_mode = None
        self.uid = 0

    def sbuf(self, name, shape, dtype):
        return Buf(name, self.nc.alloc_sbuf_tensor(name, list(shape), dtype))

    def psum(self, name, shape, dtype=F32):
        return Buf(name, self.nc.alloc_psum_tensor(name, list(shape), dtype))

    def dram(self, name, shape, dtype, kind="Internal"):
        return Buf(name, self.nc.dram_tensor(name, list(shape), dtype, kind=kind))

    def view(self, name, t):
        return Buf(name, t)

    def _check(self, e, tok, same_engine_ok):
        if tok is None:
            return None
        sem, val, teng = tok
        if teng == e and same_engine_ok:
            return None
        key = id(sem)
        if self.waited[e].get(key, 0) >= val:
            return None
        self.waited[e][key] = val
        return (sem, val)

    def _need(self, e, tok, same_engine_ok):
        w = self._check(e, tok, same_engine_ok)
        if w is not None:
            self.eng[e].wait_ge(w[0], w[1])
            self.nwaits += 1

    def _collect(self, e, reads, writes, is_dma=False):
        out = []
        for i, b in enumerate(reads):
            w = self._check(e, b.last_w, False)
            if w is not None:
                out.append((w[0], w[1], i == 0))
        for b in writes:
            w = self._check(e, b.last_w, not is_dma)
            if w is not None:
                out.append((w[0], w[1], False))
            for r in b.readers:
                w = self._check(e, r, not is_dma)
                if w is not None:
                    out.append((w[0], w[1], False))
        return out

    def _deps(self, e, reads, writes, is_dma=False):
        for (sem, val, _) in self._collect(e, reads, writes, is_dma):
            self.eng[e].wait_ge(sem, val)
            self.nwaits += 1

    def _record(self, tok, reads, writes):
        for b in writes:
            b.last_w = tok
            b.readers = []
        for b in reads:
            b.readers.append(tok)
            if len(b.readers) > 16:
                best = {}
                for t in b.readers:
                    k = id(t[0])
                    if k not in best or best[k][1] < t[1]:
                        best[k] = t
                b.readers = list(best.values())

    def op(self, e, fn, reads=(), writes=(), mode=None, attach=True):
        if e == "pe":
            if mode != self.pe_mode and self.cnt["pe"] > 0:
                self._need("pe", (self.sem["pe"], self.cnt["pe"], "x"), False)
            self.pe_mode = mode
        waits = self._collect(e, reads, writes)
        att = None
        if attach and ATTACH_WAITS:
            for i in range(len(waits) - 1, -1, -1):
                if not (e == "pe" and waits[i][2]):
                    att = waits.pop(i)
                    break
        for (sem, val, _) in waits:
            self.eng[e].wait_ge(sem, val)
            self.nwaits += 1
        ins = fn()
        if att is not None:
            ins._wait_ge(att[0], att[1])
        self.cnt[e] += 1
        ins.then_inc(self.sem[e], 1)
        tok = (self.sem[e], self.cnt[e], e)
        self._record(tok, reads, writes)
        self.nins += 1
        return ins

    def dma(self, q, out_ap, in_ap, reads=(), writes=(), sembuf=None, **kw):
        self._deps(q, reads, writes, is_dma=True)
        sb = sembuf if sembuf is not None else (writes[0] if writes else reads[0])
        if sb.sem is None:
            self.uid += 1
            sb.sem = self.nc.alloc_semaphore("dma%d_%s" % (self.uid, sb.name))
        ins = self.eng[q].dma_start(out=out_ap, in_=in_ap, **kw)
        sb.semcnt += 16
        ins.then_inc(sb.sem, 16)
        tok = (sb.sem, sb.semcnt, "dma")
        self._record(tok, reads, writes)
        self.nins += 1
        return ins

    def finish(self, bufs, e="sp"):
        for b in bufs:
            self._need(e, b.last_w, False)


class Rot:
    def __init__(self, bufs):
        self.bufs = bufs
        self.i = 0
        self.held = {}
        self.owner = None

    def get(self):
        for _ in range(len(self.bufs)):
            b = self.bufs[self.i % len(self.bufs)]
            self.i += 1
            if id(b) not in self.held:
                if self.owner is not None:
                    self.held[id(b)] = self.owner
                return b
        raise RuntimeError("buffer pool exhausted")

    def release_owner(self, owner):
        self.held = {k: v for k, v in self.held.items() if v != owner}


def build(NT, NL, T):
    NB = T // 128
    SEQ = NT * T
    NKB = SEQ // 128
    nc = bass.Bass("TRN2", target_bir_lowering=False)
    S = Sched(nc)
    nv = nc.vector
    na = nc.scalar
    npool = nc.gpsimd
    npe = nc.tensor

    x_in = S.dram("x", [SEQ, D], F32, kind="ExternalInput")
    pos_in = S.dram("positions", [1, SEQ], I32, kind="ExternalInput")
    invf_in = S.dram("invf", [96, 1], F32, kind="ExternalInput")
    W = {}
    for nm, shp in [("pre_norm_g", [4, D]), ("w_in", [4, D, DIN]), ("conv_w", [4, 4, LW]), ("conv_b", [4, LW]),
                    ("lru_wa", [4, 6, 64, 64]), ("lru_ba", [4, LW]), ("lru_wx", [4, 6, 64, 64]), ("lru_bx", [4, LW]),
                    ("lru_lambda", [4, LW]), ("q_norm_g", [4, 192]), ("w_uq", [4, 192, 576]), ("kv_norm_g", [4, 128]),
                    ("w_ukv", [4, 128, 768]), ("sgu_norm_g", [4, 256]), ("sgu_norm_b", [4, 256]),
                    ("sgu_w", [4, 4, 128, 128]), ("sgu_b", [4, 4, 128]), ("branch_norm_g", [4, D]),
                    ("w_out", [4, D, D]), ("post_norm_g", [4, D])]:
        W[nm] = S.dram(nm, shp, F32, kind="ExternalInput")
    out_d = S.dram("out", [SEQ, D], F32, kind="ExternalOutput")
    xs_d = [S.dram("xs%d" % i, [SEQ, D], F32) for i in range(2)]
    tab_d = S.dram("tabs", [2, 96, SEQ], F32)
    KCH = 1024
    NCH = (SEQ + KCH - 1) // KCH
    kc_d = S.dram("kcache", [NL, 2, 96, NCH, 3, KCH], BF16)
    vc_d = S.dram("vcache", [NL, 2, 128, NKB, 3 * 65], BF16)
    xreg = {}
    def xr(key):
        if key not in xreg:
            xreg[key] = Buf("xr%s" % (key,))
        return xreg[key]
    tabreg = [Buf("tab%d" % j) for j in range(NT)]
    kreg = {}
    vreg = {}

    ident = S.sbuf("ident", [128, 128], BF16)
    ones = S.sbuf("ones", [128, 128], BF16)
    esel = S.sbuf("esel", [128, 128], BF16)
    invf = S.sbuf("invf_sb", [96, 1], F32)
    win = S.sbuf("win", [128, 8, WCOLS], BF16)
    wout = S.sbuf("wout", [128, 8, D], BF16)
    wuq = S.sbuf("wuq", [96, 2, 576], BF16)
    wuqr = S.sbuf("wuqr", [96, 2, 576], BF16)
    wukv = S.sbuf("wukv", [128, 800], BF16)
    wv = S.sbuf("wv", [128, 384], BF16)
    wabd = S.sbuf("wabd", [128, 3, 128], BF16)
    wxbd = S.sbuf("wxbd", [128, 3, 128], BF16)
    wsT = S.sbuf("wsT", [128, 4, 128], BF16)
    wstmp = S.sbuf("wstmp", [128, 4, 128], BF16)
    pv = S.sbuf("pvec", [128, 64], F32)
    PV_GPRE, PV_BG, PV_CW, PV_CB, PV_BA, PV_BX, PV_LAM, PV_CC, PV_QG, PV_KVG, PV_TMP = 0, 8, 16, 28, 31, 34, 37, 40, 46, 48, 50
    PV_NBA, PV_NBX = 54, 57
    gpost = S.sbuf("gpost", [128, D], F32)
    sgn_g = S.sbuf("sgn_g", [128, 256], F32)
    sgn_b = S.sbuf("sgn_b", [128, 256], F32)
    bsbc = S.sbuf("bsbc", [128, 2, T], F32)
    xt_pool = Rot([S.sbuf("xt%d" % i, [128, NB * D], F32) for i in range(2)])
    stg_pool = Rot([S.sbuf("stg%d" % i, [128, 1152], F32) for i in range(2)])
    hb_pool = Rot([S.sbuf("hb%d" % i, [128, D], BF16) for i in range(4)])
    hT = S.sbuf("hT", [128, 8, T], BF16)
    yT = S.sbuf("yT", [128, 8, T], BF16)
    yf = S.sbuf("yf", [128, 8, T], F32)
    sgb = S.sbuf("sgb", [128, 3, T], F32)
    ugc = S.sbuf("ugc", [128, 2, T], F32)
    xa_sb3 = [S.sbuf("xa_sb%d" % i, [128, T + 3], F32) for i in range(3)]
    hs_pool = Rot([S.sbuf("hs%d" % i, [128, 3, T], F32) for i in range(2)])
    ql = S.sbuf("ql", [96, 2, T], F32)
    qln = S.sbuf("qln", [96, 2, T], BF16)
    qT = S.sbuf("qT", [96, NH, T], BF16)
    kcur = S.sbuf("kcur", [96, NH, T], BF16)
    vcur = S.sbuf("vcur", [128, NB, NH, 65], BF16)
    ckvT = S.sbuf("ckvT", [128, T], BF16)
    qT_h = [Buf("qT_h%d" % h, qT.t) for h in range(NH)]
    yT_c = [Buf("yT_c%d" % i, yT.t) for i in range(2)]
    yT_b = Buf("yT_b", yT.t)
    yfA, yfB, yfC = Buf("yfA", yf.t), Buf("yfB", yf.t), Buf("yfC", yf.t)
    kcur_h = [Buf("kcur_h%d" % h, kcur.t) for h in range(NH)]
    vcur_b = [Buf("vcur_b%d" % b, vcur.t) for b in range(NB)]
    zhi3 = [S.sbuf("zhi_%d" % i, [128, T], BF16) for i in range(3)]
    zlo3 = [S.sbuf("zlo_%d" % i, [128, T], BF16) for i in range(3)]
    vnA2 = [S.sbuf("vnA_%d" % i, [128, 256], BF16) for i in range(NB)]
    vnB2 = [S.sbuf("vnB_%d" % i, [128, 256], BF16) for i in range(NB)]
    kpe = S.sbuf("kpe", [96, T], BF16)
    ctab = S.sbuf("ctab", [96, T], F32)
    stab = S.sbuf("stab", [96, T], F32)
    gvb = S.sbuf("gvb", [128, NB, 256], F32)
    kb_pool = Rot([S.sbuf("kbuf%d" % i, [96, 3, KCH], BF16) for i in range(2)])
    vb_pool = Rot([S.sbuf("vbuf%d" % i, [128, KCH // 128, 3 * 65], BF16) for i in range(2)])
    Tf = Rot([S.sbuf("tf%d" % i, [128, T], F32) for i in range(17)])
    Tb = Rot([S.sbuf("tb%d" % i, [128, T], BF16) for i in range(6)])
    Pp = Rot([S.sbuf("pT%d" % i, [128, 512], BF16) for i in range(6)])
    st_pool = Rot([S.sbuf("st%d" % i, [128, 16], F32) for i in range(6)])
    ps_s = [S.psum("ps_s%d" % i, [128, 512]) for i in range(2)]
    ps_o = [S.psum("ps_o%d" % i, [128, 512]) for i in range(3)]
    ps_g = [S.psum("ps_g%d" % i, [128, 512]) for i in range(2)]
    ps_t = S.psum("ps_t", [128, 1024], BF16)
    PG = Rot(ps_g + ps_s)
    PSr = Rot(ps_s)
    PS4 = Rot(ps_g + ps_s)
    ps_t32 = Alias(ps_t, ps_t.t[:, :].bitcast(F32))
    PS5 = Rot(ps_g + ps_s + [ps_t32])

    def ACT(out, in_, func, reads, writes, **kw):
        return S.op("act", lambda: na.activation(out=out, in_=in_, func=func, **kw), reads, writes,
                    attach=("accum_out" not in kw))

    def TT(e, out, in0, in1, op, reads, writes):
        eng = nv if e == "dve" else npool
        return S.op(e, lambda: eng.tensor_tensor(out=out, in0=in0, in1=in1, op=op), reads, writes)

    def TS(e, out, in0, s1, s2, op0, op1, reads, writes):
        eng = nv if e == "dve" else npool
        if s2 is None:
            return S.op(e, lambda: eng.tensor_scalar(out=out, in0=in0, scalar1=s1, scalar2=None, op0=op0), reads, writes)
        return S.op(e, lambda: eng.tensor_scalar(out=out, in0=in0, scalar1=s1, scalar2=s2, op0=op0, op1=op1), reads, writes)

    def STT(out, in0, sc, in1, op0, op1, reads, writes):
        return S.op("dve", lambda: nv.scalar_tensor_tensor(out=out, in0=in0, scalar=sc, in1=in1, op0=op0, op1=op1), reads, writes)

    def CP(e, out, in_, reads, writes):
        if e == "act":
            return S.op("act", lambda: na.copy(out=out, in_=in_), reads, writes)
        eng = nv if e == "dve" else npool
        return S.op(e, lambda: eng.tensor_copy(out=out, in_=in_), reads, writes)

    def MM(out, lhsT, rhs, start, stop, reads, writes):
        return S.op("pe", lambda: npe.matmul(out, lhsT=lhsT, rhs=rhs, start=start, stop=stop), reads, writes)

    def MSET(e, ap, val, writes):
        eng = nv if e == "dve" else npool
        return S.op(e, lambda: eng.memset(ap, val), (), writes)

    def rstd_from(ps_ap, n, out_ap, reads, writes):
        ACT(out_ap, ps_ap, AF.Ln, reads, writes, scale=1.0 / n, bias=EPS)
        ACT(out_ap, out_ap, AF.Exp, writes, writes, scale=-0.5)

    tmpf = Tf.get()
    MSET("pool", tmpf[:, 0:128], 0.0, [tmpf])
    S.op("pool", lambda: npool.affine_select(out=tmpf[:, 0:128], in_=tmpf[:, 0:128], pattern=[[-1, 128]],
                                             compare_op=ALU.not_equal, fill=1.0, base=0, channel_multiplier=1),
         [tmpf], [tmpf])
    CP("dve", ident[:], tmpf[:, 0:128], [tmpf], [ident])
    MSET("dve", ones[:], 1.0, [ones])
    MSET("dve", esel[:], 0.0, [esel])
    MSET("dve", esel[64:65, :], 1.0, [esel])
    for i3 in range(3):
        MSET("pool", zhi3[i3][:], 0.0, [zhi3[i3]])
        MSET("pool", zlo3[i3][:], 0.0, [zlo3[i3]])
    MSET("pool", vcur[:], 1.0, vcur_b)
    MSET("pool", win[:], 0.0, [win])
    MSET("pool", wuqr[:], 0.0, [wuqr])
    MSET("pool", wukv[:], 0.0, [wukv])
    for i3 in range(NB):
        MSET("dve", vnA2[i3][:], 0.0, [vnA2[i3]])
        MSET("dve", vnB2[i3][:], 0.0, [vnB2[i3]])
    S.dma("sp", invf[:], invf_in[:], reads=[invf_in], writes=[invf])

    def run_rr(gens, W, pools=(), stagger=False):
        gens = list(gens)
        active = []
        while gens or active:
            if stagger:
                if gens and len(active) < W:
                    active.append(gens.pop(0))
            else:
                while gens and len(active) < W:
                    active.append(gens.pop(0))
            for g in list(active):
                for p in pools:
                    p.owner = id(g)
                try:
                    next(g)
                except StopIteration:
                    active.remove(g)
                    for p in pools:
                        p.release_owner(id(g))
                for p in pools:
                    p.owner = None

    C1 = 6.28125
    C2 = 2.0 * np.pi - 6.28125
    def g_tab(j, which):
        b0, b1, b2 = Tf.get(), Tf.get(), Tf.get()
        S.dma("sp", b0[0:96, :].bitcast(I32), pos_in[0:1, j * T:(j + 1) * T].to_broadcast([96, T]), reads=[pos_in], writes=[b0])
        yield
        CP("dve", b1[0:96, :], b0[0:96, :].bitcast(I32), [b0], [b1])
        yield
        if which == 0:
            TS("dve", b1[0:96, :], b1[0:96, :], invf[:, 0:1], float(np.pi / 2), ALU.mult, ALU.add, [b1, invf], [b1])
        else:
            TS("dve", b1[0:96, :], b1[0:96, :], invf[:, 0:1], None, ALU.mult, None, [b1, invf], [b1])
        yield
        TS("dve", b0[0:96, :].bitcast(I32), b1[0:96, :], float(1.0 / (2 * np.pi)), None, ALU.mult, None, [b1], [b0])
        yield
        CP("dve", b2[0:96, :], b0[0:96, :].bitcast(I32), [b0], [b2])
        yield
        STT(b0[0:96, :], b2[0:96, :], -C1, b1[0:96, :], ALU.mult, ALU.add, [b2, b1], [b0])
        yield
        STT(b1[0:96, :], b2[0:96, :], -C2, b0[0:96, :], ALU.mult, ALU.add, [b2, b0], [b1])
        yield
        TS("dve", b1[0:96, :], b1[0:96, :], 3.1415925, -3.1415925, ALU.min, ALU.max, [b1], [b1])
        yield
        ACT(b2[0:96, :], b1[0:96, :], AF.Sin, [b1], [b2])
        yield
        S.dma("sp", tab_d[which, :, j * T:(j + 1) * T], b2[0:96, :], reads=[b2], writes=[tabreg[j]], sembuf=b2)
        yield

    run_rr([g_tab(j, w_) for j in range(NT) for w_ in range(2)], 5, [Tf], stagger=True)

    def load_vec(dst_ap, src_ap, reads):
        S.dma("sp", dst_ap, src_ap, reads=reads, writes=[pv], allow_slow_non_contiguous=True)

    def prep_layer(l):
        load_vec(pv[:, PV_GPRE:PV_GPRE + 8], W["pre_norm_g"][l, :].rearrange("(c p) -> p c", p=128), [W["pre_norm_g"]])
        load_vec(pv[:, PV_BG:PV_BG + 8], W["branch_norm_g"][l, :].rearrange("(c p) -> p c", p=128), [W["branch_norm_g"]])
        for k in range(4):
            load_vec(pv[:, PV_CW + 3 * k:PV_CW + 3 * k + 3], W["conv_w"][l, k, :].rearrange("(c p) -> p c", p=128), [W["conv_w"]])
        for nm, off in [("conv_b", PV_CB), ("lru_ba", PV_BA), ("lru_bx", PV_BX), ("lru_lambda", PV_LAM)]:
            load_vec(pv[:, off:off + 3], W[nm][l, :].rearrange("(c p) -> p c", p=128), [W[nm]])
        load_vec(pv[0:96, PV_QG:PV_QG + 2], W["q_norm_g"][l, :].rearrange("(c p) -> p c", p=96), [W["q_norm_g"]])
        load_vec(pv[:, PV_KVG:PV_KVG + 1], W["kv_norm_g"][l, :].rearrange("(c p) -> p c", p=128), [W["kv_norm_g"]])
        TS("dve", pv[:, PV_NBA:PV_NBA + 3], pv[:, PV_BA:PV_BA + 3], -1.0, None, ALU.mult, None, [pv], [pv])
        TS("dve", pv[:, PV_NBX:PV_NBX + 3], pv[:, PV_BX:PV_BX + 3], -1.0, None, ALU.mult, None, [pv], [pv])
        ACT(pv[:, PV_TMP:PV_TMP + 3], pv[:, PV_LAM:PV_LAM + 3], AF.Exp, [pv], [pv], scale=-1.0)
        ACT(pv[:, PV_TMP:PV_TMP + 3], pv[:, PV_TMP:PV_TMP + 3], AF.Ln, [pv], [pv], bias=1.0)
        TS("dve", pv[:, PV_CC:PV_CC + 3], pv[:, PV_TMP:PV_TMP + 3], -8.0, None, ALU.mult, None, [pv], [pv])
        TS("dve", pv[:, PV_CC + 3:PV_CC + 6], pv[:, PV_TMP:PV_TMP + 3], -16.0, None, ALU.mult, None, [pv], [pv])
        S.dma("sp", gpost[:], W["post_norm_g"][l:l + 1, :].to_broadcast([128, D]), reads=[W["post_norm_g"]], writes=[gpost])
        S.dma("sp", sgn_g[:], W["sgu_norm_g"][l:l + 1, :].to_broadcast([128, 256]), reads=[W["sgu_norm_g"]], writes=[sgn_g])
        S.dma("sp", sgn_b[:], W["sgu_norm_b"][l:l + 1, :].to_broadcast([128, 256]), reads=[W["sgu_norm_b"]], writes=[sgn_b])
        for cc in range(2):
            for gi in range(2):
                for b in range(NB):
                    S.dma("sp", bsbc[gi * 64:(gi + 1) * 64, cc, b * 128:(b + 1) * 128],
                          W["sgu_b"][l, 2 * cc + gi:2 * cc + gi + 1, :].to_broadcast([64, 128]),
                          reads=[W["sgu_b"]], writes=[bsbc])
        for c in range(8):
            g = pv[:, PV_GPRE + c:PV_GPRE + c + 1]
            e = "dve"
            stg = stg_pool.get()
            S.dma("sp", stg[:, 0:1120], W["w_in"][l, c * 128:(c + 1) * 128, 0:1120], reads=[W["w_in"]], writes=[stg])
            TS(e, win[:, c, 0:1088], stg[:, 0:1088], g, None, ALU.mult, None, [stg, pv], [win])
            TS(e, win[:, c, C_KR1 + 64:C_KR1 + 96], stg[:, 1088:1120], g, None, ALU.mult, None, [stg, pv], [win])
            TS(e, win[:, c, C_KR2 + 64:C_KR2 + 80], stg[:, 1104:1120], g, -1.0, ALU.mult, ALU.mult, [stg, pv], [win])
            TS(e, win[:, c, C_KR2 + 80:C_KR2 + 96], stg[:, 1088:1104], g, None, ALU.mult, None, [stg, pv], [win])
            stg = stg_pool.get()
            S.dma("sp", stg[:, 0:1152], W["w_in"][l, c * 128:(c + 1) * 128, 1120:DIN], reads=[W["w_in"]], writes=[stg])
            TS(e, win[:, c, C_GB:WCOLS], stg[:, 0:1152], g, None, ALU.mult, None, [stg, pv], [win])
        for c in range(8):
            stg = stg_pool.get()
            S.dma("sp", stg[:, 0:D], W["w_out"][l, c * 128:(c + 1) * 128, :], reads=[W["w_out"]], writes=[stg])
            TS("dve", wout[:, c, :], stg[:, 0:D], pv[:, PV_BG + c:PV_BG + c + 1], None, ALU.mult, None, [stg, pv], [wout])
        stg = stg_pool.get()
        S.dma("sp", stg[0:96, 0:1152].rearrange("p (c n) -> p c n", c=2),
              W["w_uq"][l, :, :].rearrange("(c p) n -> p c n", p=96), reads=[W["w_uq"]], writes=[stg])
        for c in range(2):
            g = pv[0:96, PV_QG + c:PV_QG + c + 1]
            src = stg[0:96, c * 576:(c + 1) * 576]
            TS("dve", wuq[:, c, :], src, g, None, ALU.mult, None, [stg, pv], [wuq])
            s3 = src.rearrange("p (h d) -> p h d", d=96)
            d3 = wuqr[:, c, :].rearrange("p (h d) -> p h d", d=96)
            TS("dve", d3[:, :, 64:80], s3[:, :, 80:96], g, -1.0, ALU.mult, ALU.mult, [stg, pv], [wuqr])
            TS("dve", d3[:, :, 80:96], s3[:, :, 64:80], g, None, ALU.mult, None, [stg, pv], [wuqr])
        stg = stg_pool.get()
        S.dma("sp", stg[:, 0:768], W["w_ukv"][l, :, :], reads=[W["w_ukv"]], writes=[stg])
        g = pv[:, PV_KVG:PV_KVG + 1]
        TS("dve", wukv[:, 0:768], stg[:, 0:768], g, None, ALU.mult, None, [stg, pv], [wukv])
        TS("dve", wv[:, :].rearrange("p (h d) -> p h d", d=64),
           stg[:, 0:768].rearrange("p (h d) -> p h d", d=128)[:, :, 64:128], g, None, ALU.mult, None, [stg, pv], [wv])
        stg = stg_pool.get()
        MSET("dve", stg[:, 0:768], 0.0, [stg])
        for wi, nm in enumerate(["lru_wa", "lru_wx"]):
            for c in range(3):
                for hh in range(2):
                    base = wi * 384 + c * 128
                    S.dma("sp", stg[hh * 64:(hh + 1) * 64, base + hh * 64:base + hh * 64 + 64], W[nm][l, 2 * c + hh, :, :],
                          reads=[W[nm]], writes=[stg])
        CP("dve", wabd[:, :, :], stg[:, 0:384].rearrange("p (c n) -> p c n", c=3), [stg], [wabd])
        CP("dve", wxbd[:, :, :], stg[:, 384:768].rearrange("p (c n) -> p c n", c=3), [stg], [wxbd])
        stg = stg_pool.get()
        S.dma("sp", stg[:, 0:512].rearrange("p (g j) -> p g j", g=4),
              W["sgu_w"][l, :, :, :].rearrange("g i j -> i g j"), reads=[W["sgu_w"]], writes=[stg])
        MSET("dve", stg[0:64, 0:512].rearrange("p (g j) -> p g j", g=4)[:, :, 64:128], 0.0, [stg])
        CP("dve", wstmp[:, :, :], stg[:, 0:512].rearrange("p (g j) -> p g j", g=4), [stg], [wstmp])
        for g4 in range(4):
            S.op("pe", lambda: npe.transpose(ps_t[:, g4 * 128:(g4 + 1) * 128], wstmp[:, g4, :], ident[:]), [wstmp, ident], [ps_t], attach=False)
        CP("dve", wsT[:, :, :], ps_t[:, 0:512].rearrange("p (g i) -> p g i", g=4), [ps_t], [wsT])

    def branch_rms(ch0, nch, nfeat):
        pss = PG.get()
        for i in range(nch):
            sq = Tb.get()
            TT("pool", sq[:], yf[:, ch0 + i, :], yf[:, ch0 + i, :], ALU.mult, [yf], [sq])
            MM(pss[:, 0:T], ones[:], sq[:], i == 0, i == nch - 1, [ones, sq], [pss])
        rs = Tf.get()
        rstd_from(pss[:, 0:T], nfeat, rs[:], [pss], [rs])
        for i in range(nch):
            TT("dve", yT[:, ch0 + i, :], yf[:, ch0 + i, :], rs[:], ALU.mult, [yf, rs], [yT])

    state = {"hs_prev": None}

    def p0_start(j, src_ap_fn, src_reg):
        xt = xt_pool.get()
        S.dma("sp", xt[:, :].rearrange("p (b d) -> p b d", b=NB),
              src_ap_fn(j).rearrange("(b p) d -> p b d", p=128), reads=[src_reg], writes=[xt])
        st = st_pool.get()

        def tabs():
            S.dma("sp", ctab[:], tab_d[0, :, j * T:(j + 1) * T], reads=[tabreg[j]], writes=[ctab])
            S.dma("sp", stab[:], tab_d[1, :, j * T:(j + 1) * T], reads=[tabreg[j]], writes=[stab])

        def g_p0(b):
            if b == 0:
                tabs()
            junk = hb_pool.get()
            ACT(junk[:], xt[:, b * D:(b + 1) * D], AF.Square, [xt], [junk, st], accum_out=st[:, b:b + 1])
            yield
            ACT(st[:, 8 + b:9 + b], st[:, b:b + 1], AF.Ln, [st], [st], scale=1.0 / D, bias=EPS)
            yield
            ACT(st[:, 8 + b:9 + b], st[:, 8 + b:9 + b], AF.Exp, [st], [st], scale=-0.5)
            yield
            hb = hb_pool.get()
            ACT(hb[:], xt[:, b * D:(b + 1) * D], AF.Copy, [xt, st], [hb], scale=st[:, 8 + b:9 + b])
            yield
            for c in range(8):
                S.op("pe", lambda: npe.transpose(ps_t[:, c * 128:(c + 1) * 128], hb[:, c * 128:(c + 1) * 128], ident[:]),
                     [hb, ident], [ps_t], attach=False)
            CP("act" if b % 2 == 0 else "dve", hT[:, :, b * 128:(b + 1) * 128],
               ps_t[:, :].rearrange("p (c n) -> p c n", c=8), [ps_t], [hT])
            yield

        return xt, [g_p0(b) for b in range(NB)]

    def tile_layer(l, j, src_ap_fn, src_reg, dst_ap_fn, dst_reg, kvwrite, pre=None, nxt=None):
        if pre is None:
            xt, gens0 = p0_start(j, src_ap_fn, src_reg)
            run_rr(gens0, NB, [hb_pool])
        else:
            xt = pre
        result = {"next_xt": None}

        ps_t_f = "pst"

        def proj(col0, m, bank=None):
            if bank == "pst":
                v = ps_t[:, :].bitcast(F32)
                for c in range(8):
                    MM(v[0:m, 0:T], win[:, c, col0:col0 + m], hT[:, c, :], c == 0, c == 7, [win, hT], [ps_t])
                return ps_t
            ps = bank if bank is not None else PG.get()
            for c in range(8):
                MM(ps[0:m, 0:T], win[:, c, col0:col0 + m], hT[:, c, :], c == 0, c == 7, [win, hT], [ps])
            return ps

        pst_f = ps_t[:, :].bitcast(F32)

        def lat_gen():
            pss = ps_o[2]
            sq0, sq1, sqk = Tb.get(), Tb.get(), Tb.get()
            rs, kvl = Tf.get(), Tf.get()
            proj(C_QL, 96, bank=ps_t_f)
            yield
            CP("act", ql[:, 0, :], pst_f[0:96, 0:T], [ps_t], [ql])
            yield
            proj(C_QL + 96, 96, bank=ps_t_f)
            TT("pool", sq0[0:96, :], ql[:, 0, :], ql[:, 0, :], ALU.mult, [ql], [sq0])
            yield
            CP("act", ql[:, 1, :], pst_f[0:96, 0:T], [ps_t], [ql])
            MM(pss[0:96, 0:T], ones[0:96, 0:96], sq0[0:96, :], True, False, [ones, sq0], [pss])
            yield
            TT("pool", sq1[0:96, :], ql[:, 1, :], ql[:, 1, :], ALU.mult, [ql], [sq1])
            proj(C_KV, 128, bank=ps_t_f)
            yield
            MM(pss[0:96, 0:T], ones[0:96, 0:96], sq1[0:96, :], False, True, [ones, sq1], [pss])
            CP("act", kvl[:], pst_f[:, 0:T], [ps_t], [kvl])
            yield
            ACT(rs[0:96, :], pss[0:96, 0:T], AF.Ln, [pss], [rs], scale=1.0 / 192.0, bias=EPS)
            TT("pool", sqk[:], kvl[:], kvl[:], ALU.mult, [kvl], [sqk])
            yield
            ACT(rs[0:96, :], rs[0:96, :], AF.Exp, [rs], [rs], scale=-0.5)
            yield
            for c in range(2):
                TT("dve", qln[:, c, :], ql[:, c, :], rs[0:96, :], ALU.mult, [ql, rs], [qln])
            MM(pss[:, 0:T], ones[:], sqk[:], True, True, [ones, sqk], [pss])
            yield
            ACT(rs[:], pss[:, 0:T], AF.Ln, [pss, qln], [rs], scale=1.0 / 128.0, bias=EPS)
            yield
            ACT(rs[:], rs[:], AF.Exp, [rs], [rs], scale=-0.5)
            yield
            TT("dve", ckvT[:], kvl[:], rs[:], ALU.mult, [kvl, rs], [ckvT])
            yield

        hs = hs_pool.get()
        hs_prev = state["hs_prev"]
        LB = [ps_g[0], ps_g[1], ps_s[0], ps_s[1], ps_o[0], ps_o[1]]

        def lru_gen(c):
            cv, ga_, gx_, a_, m_ = Tf.get(), Tf.get(), Tf.get(), Tf.get(), Tf.get()
            cvb = Tb.get()
            psa, psx = LB[2 * c], LB[2 * c + 1]
            ps = proj(C_XA + c * 128, 128, bank=psa)
            CP("act", xa_sb3[c][:, 3:T + 3], ps[:, 0:T], [ps], [xa_sb3[c]])
            yield
            cw = lambda k: pv[:, PV_CW + 3 * k + c:PV_CW + 3 * k + c + 1]
            xc = xa_sb3[c]
            TS("dve", cv[:], xc[:, 0:T], cw(0), pv[:, PV_CB + c:PV_CB + c + 1], ALU.mult, ALU.add, [xc, pv], [cv])
            for k in range(1, 4):
                STT(cv[:], xc[:, k:k + T], cw(k), cv[:], ALU.mult, ALU.add, [xc, pv, cv], [cv])
            yield
            CP("pool", xc[:, 0:3], xc[:, T:T + 3], [xc], [xc])
            CP("act", cvb[:], cv[:], [cv], [cvb])
            yield
            MM(psa[:, 0:T], wabd[:, c, :], cvb[:], True, True, [wabd, cvb], [psa])
            MM(psx[:, 0:T], wxbd[:, c, :], cvb[:], True, True, [wxbd, cvb], [psx])
            yield
            ACT(ga_[:], psa[:, 0:T], AF.Exp, [psa, pv], [ga_], scale=-1.0, bias=pv[:, PV_NBA + c:PV_NBA + c + 1])
            ACT(gx_[:], psx[:, 0:T], AF.Exp, [psx, pv], [gx_], scale=-1.0, bias=pv[:, PV_NBX + c:PV_NBX + c + 1])
            yield
            ACT(ga_[:], ga_[:], AF.Ln, [ga_], [ga_], bias=1.0)
            ACT(gx_[:], gx_[:], AF.Ln, [gx_], [gx_], bias=1.0)
            yield
            ACT(ga_[:], ga_[:], AF.Exp, [ga_], [ga_], scale=-1.0)
            ACT(gx_[:], gx_[:], AF.Exp, [gx_], [gx_], scale=-1.0)
            yield
            ACT(a_[:], ga_[:], AF.Exp, [ga_, pv], [a_], scale=pv[:, PV_CC + c:PV_CC + c + 1])
            ACT(m_[:], ga_[:], AF.Exp, [ga_, pv], [m_], scale=pv[:, PV_CC + 3 + c:PV_CC + 4 + c])
            yield
            TS("dve", m_[:], m_[:], -1.0, 1.0, ALU.mult, ALU.add, [m_], [m_])
            yield
            ACT(m_[:], m_[:], AF.Ln, [m_], [m_])
            TT("dve", gx_[:], gx_[:], cv[:], ALU.mult, [gx_, cv], [gx_])
            yield
            ACT(m_[:], m_[:], AF.Exp, [m_], [m_], scale=0.5)
            yield
            TT("dve", gx_[:], gx_[:], m_[:], ALU.mult, [gx_, m_], [gx_])
            init = 0.0 if hs_prev is None else hs_prev[:, c, T - 1:T]
            S.op("dve", lambda: nv.tensor_tensor_scan(out=hs[:, c, :], data0=a_[:], data1=gx_[:], initial=init,
                                                      op0=ALU.mult, op1=ALU.add),
                 [a_, gx_] + ([hs_prev] if hs_prev is not None else []), [hs])
            yield

        run_rr([lat_gen()] + [lru_gen(c) for c in range(3)], 4, [Tf, Tb], stagger=True)
        state["hs_prev"] = hs

        def silu2(ps_ap, psbuf, dst_ap, dstbuf):
            t_ = Tf.get()
            ACT(t_[:], ps_ap, AF.Tanh, [psbuf], [t_], scale=0.5)
            STT(dst_ap, t_[:], 1.0, ps_ap, ALU.add, ALU.mult, [t_, psbuf], [dstbuf])

        def gelu2(ps_ap, psbuf, dst_ap, dstbuf, w):
            t_ = Tf.get()
            ACT(t_[:, 0:w], ps_ap, AF.Square, [psbuf], [t_])
            TS("dve", t_[:, 0:w], t_[:, 0:w], 0.044715, 1.0, ALU.mult, ALU.add, [t_], [t_])
            TT("dve", t_[:, 0:w], t_[:, 0:w], ps_ap, ALU.mult, [t_, psbuf], [t_])
            ACT(t_[:, 0:w], t_[:, 0:w], AF.Tanh, [t_], [t_], scale=0.7978845608028654)
            STT(dst_ap, t_[:, 0:w], 1.0, ps_ap, ALU.add, ALU.mult, [t_, psbuf], [dstbuf])

        BB = Rot([ps_g[0], ps_g[1], ps_s[0], ps_s[1], ps_o[0], ps_o[1], ps_o[2]])

        def grp_proj(col0, m, bk):
            for c in range(8):
                MM(bk[0:m, 0:T], win[:, c, col0:col0 + m], hT[:, c, :], c == 0, c == 7, [win, hT], [bk])

        def tanh_half(bk, w):
            t_ = Tf.get()
            ACT(t_[:, 0:w], bk[:, 0:w], AF.Tanh, [bk], [t_], scale=0.5)
            return t_

        def gelu_stages(bk, w, dst_ap, dstbuf):
            t_ = Tf.get()
            ACT(t_[:, 0:w], bk[:, 0:w], AF.Square, [bk], [t_])
            yield
            TS("dve", t_[:, 0:w], t_[:, 0:w], 0.044715, 1.0, ALU.mult, ALU.add, [t_], [t_])
            TT("dve", t_[:, 0:w], t_[:, 0:w], bk[:, 0:w], ALU.mult, [t_, bk], [t_])
            yield
            ACT(t_[:, 0:w], t_[:, 0:w], AF.Tanh, [t_], [t_], scale=0.7978845608028654)
            yield
            STT(dst_ap, t_[:, 0:w], 1.0, bk[:, 0:w], ALU.add, ALU.mult, [t_, bk], [dstbuf])
            yield

        def g_ga(c):
            bk = BB.get()
            grp_proj(C_GA + c * 128, 128, bk)
            yield
            t_ = tanh_half(bk, T)
            yield
            STT(t_[:], t_[:], 1.0, bk[:, 0:T], ALU.add, ALU.mult, [t_, bk], [t_])
            STT(yf[:, c, :], t_[:], 0.5, hs[:, c, :], ALU.mult, ALU.mult, [t_, hs], [yfA])
            yield

        def g_ugc(cc):
            bk = BB.get()
            grp_proj(C_U + cc * 128, 128, bk)
            yield
            gu = Tf.get()
            yield from gelu_stages(bk, T, gu[:], gu)
            bk2 = BB.get()
            grp_proj(C_GC + cc * 128, 128, bk2)
            yield
            t2 = tanh_half(bk2, T)
            yield
            STT(t2[:], t2[:], 1.0, bk2[:, 0:T], ALU.add, ALU.mult, [t2, bk2], [t2])
            STT(ugc[:, cc, :], gu[:], 0.25, t2[:], ALU.mult, ALU.mult, [gu, t2], [ugc])
            yield

        def g_v(b):
            bk = BB.get()
            for c in range(8):
                MM(bk[:, 0:256], hT[:, c, b * 128:(b + 1) * 128], win[:, c, C_V:C_V + 256], c == 0, c == 7, [hT, win], [bk])
            yield
            yield from gelu_stages(bk, 256, gvb[:, b, :], gvb)

        def g_gb(c):
            bk = BB.get()
            grp_proj(C_GB + c * 128, 128, bk)
            yield
            t_ = tanh_half(bk, T)
            yield
            STT(sgb[:, c, :], t_[:], 1.0, bk[:, 0:T], ALU.add, ALU.mult, [t_, bk], [sgb])
            yield

        def g_q(h):
            pA, pB = BB.get(), BB.get()
            for c in range(2):
                MM(pA[0:96, 0:T], wuq[:, c, h * 96:(h + 1) * 96], qln[:, c, :], c == 0, c == 1, [wuq, qln], [pA])
            for c in range(2):
                MM(pB[0:96, 0:T], wuqr[:, c, h * 96:(h + 1) * 96], qln[:, c, :], c == 0, c == 1, [wuqr, qln], [pB])
            yield
            t1, t2 = Tf.get(), Tf.get()
            TT("dve", t1[0:96, :], pA[0:96, 0:T], ctab[:], ALU.mult, [pA, ctab], [t1])
            TT("dve", t2[0:96, :], pB[0:96, 0:T], stab[:], ALU.mult, [pB, stab], [t2])
            yield
            TT("pool", qT[:, h, :], t1[0:96, :], t2[0:96, :], ALU.add, [t1, t2], [qT_h[h]])
            yield

        def g_kpe():
            p1, p2 = BB.get(), BB.get()
            grp_proj(C_KR1, 96, p1)
            grp_proj(C_KR2, 96, p2)
            yield
            t1, t2 = Tf.get(), Tf.get()
            TT("dve", t1[64:96, :], p1[64:96, 0:T], ctab[64:96, :], ALU.mult, [p1, ctab], [t1])
            TT("dve", t2[64:96, :], p2[64:96, 0:T], stab[64:96, :], ALU.mult, [p2, stab], [t2])
            yield
            TT("pool", kpe[64:96, :], t1[64:96, :], t2[64:96, :], ALU.add, [t1, t2], [kpe])
            yield

        def g_k(h):
            ps = BB.get()
            MM(ps[0:96, 0:T], wukv[:, h * 128:h * 128 + 96], ckvT[:], True, True, [wukv, ckvT], [ps])
            yield
            CP("act", kcur[0:64, h, :], ps[0:64, 0:T], [ps], [kcur_h[h]])
            CP("pool", kcur[64:96, h, :], kpe[64:96, :], [kpe], [kcur_h[h]])
            yield

        def g_vp(b):
            ps = BB.get()
            MM(ps[:, 0:384], ckvT[:, b * 128:(b + 1) * 128], wv[:], True, True, [ckvT, wv], [ps])
            yield
            CP("act", vcur[:, b, :, 0:64], ps[:, 0:384].rearrange("p (h d) -> p h d", d=64), [ps], [vcur_b[b]])
            yield

        qk = []
        for h in range(NH):
            qk += [g_q(h), g_k(h)]
        run_rr([g_ga(c) for c in range(3)] + [g_ugc(cc) for cc in range(2)] + [g_v(b) for b in range(NB)]
               + [g_kpe()] + [g_gb(c) for c in range(3)] + qk + [g_vp(b) for b in range(NB)], 4, [Tf, BB], stagger=True)
        if kvwrite:
            kr = Buf("k_%d_%d" % (l, j))
            vr = Buf("v_%d_%d" % (l, j))
            kreg[(l, j)] = kr
            vreg[(l, j)] = vr
            cj, o_ = (j * T) // KCH, (j * T) % KCH
            for hg in range(2):
                S.dma("pool", kc_d[l, hg, :, cj, :, o_:o_ + T], kcur[:, hg * 3:(hg + 1) * 3, :],
                      reads=kcur_h[hg * 3:(hg + 1) * 3], writes=[kr], sembuf=kcur)
                S.dma("pool", vc_d[l, hg, :, j * NB:(j + 1) * NB, :],
                      vcur[:, :, hg * 3:(hg + 1) * 3, :].rearrange("p b h d -> p b (h d)"),
                      reads=vcur_b, writes=[vr], sembuf=vcur)

        npre = j * NB
        scale = float(DQK ** -0.5)
        PER = 512 // T
        LAG = 4
        for hg in range(2):
            po = ps_o
            first = [True, True, True]
            loaded = {}

            def chunk(ci):
                if ci not in loaded:
                    k0 = ci * KCH
                    nk = min(KCH, npre * 128 - k0)
                    nkb = nk // 128
                    kb_ = kb_pool.get()
                    vb_ = vb_pool.get()
                    tiles_needed = sorted(set((k0 + i * 128) // T for i in range(nkb)))
                    S.dma("sp", kb_[:, :, 0:nk], kc_d[l, hg, :, ci, :, 0:nk],
                          reads=[kreg[(l, tj)] for tj in tiles_needed], writes=[kb_])
                    S.dma("sp", vb_[:, 0:nkb, :], vc_d[l, hg, :, k0 // 128:k0 // 128 + nkb, :],
                          reads=[vreg[(l, tj)] for tj in tiles_needed], writes=[vb_])
                    loaded[ci] = (kb_, vb_)
                return loaded[ci]

            items = [("p", kb) for kb in range(npre)] + [("d", b) for b in range(NB)]
            groups = [items[i:i + PER] for i in range(0, len(items), PER)]
            units = [(g, hh) for g in groups for hh in range(3)]
            pend = []

            def emit_S(g, hh):
                h = hg * 3 + hh
                bank = PS5.get()
                pt = Pp.get()
                info = []
                for slot, (kind, idx) in enumerate(g):
                    c0 = slot * T
                    if kind == "p":
                        kb_, vb_ = chunk(idx * 128 // KCH)
                        kk = idx - (idx * 128 // KCH) * (KCH // 128)
                        MM(bank[:, c0:c0 + T], kb_[:, hh, kk * 128:(kk + 1) * 128], qT[:, h, :], True, True, [kb_, qT_h[h]], [bank])
                        info.append((c0, 0, vb_[:, kk, hh * 65:(hh + 1) * 65], vb_, False))
                    else:
                        q0 = idx * 128
                        MM(bank[:, c0 + q0:c0 + T], kcur[:, h, idx * 128:(idx + 1) * 128], qT[:, h, q0:T], True, True,
                           [kcur_h[h], qT_h[h]], [bank])
                        info.append((c0, q0, vcur[:, idx, h, :], vcur_b[idx], True))
                if all(q0 == 0 for (_, q0, _, _, _) in info):
                    w = len(info) * T
                    ACT(pt[:, 0:w], bank[:, 0:w], AF.Exp, [bank], [pt], scale=scale)
                else:
                    for (c0, q0, _, _, _) in info:
                        ACT(pt[:, c0 + q0:c0 + T], bank[:, c0 + q0:c0 + T], AF.Exp, [bank], [pt], scale=scale)
                for (c0, q0, _, _, isd) in info:
                    if isd:
                        MSET("pool", pt[64:128, c0 + q0:c0 + q0 + 64], 0.0, [pt])
                return (hh, pt, info)

            def emit_PV(u, last):
                hh, pt, info = u
                for n_, (c0, q0, vap, vbuf_, isd) in enumerate(info):
                    MM(po[hh][0:65, q0:T], vap, pt[:, c0 + q0:c0 + T], first[hh], last and n_ == len(info) - 1, [vbuf_, pt], [po[hh]])
                    first[hh] = False

            for i in range(len(units) + LAG):
                if i < len(units):
                    pend.append(emit_S(*units[i]))
                if i - LAG >= 0:
                    emit_PV(pend[i - LAG], i - LAG >= len(units) - 3)
            def g_norm(hh):
                h = hg * 3 + hh
                zhi, zlo = zhi3[hh], zlo3[hh]
                TS("dve", zhi[64:65, :], po[hh][64:65, 0:T], 1.0, None, ALU.mult, None, [po[hh]], [zhi])
                TT("dve", zlo[64:65, :], po[hh][64:65, 0:T], zhi[64:65, :], ALU.subtract, [po[hh], zhi], [zlo])
                yield
                pb = PG.get()
                MM(pb[:, 0:T], esel[:], zhi[:], True, False, [esel, zhi], [pb])
                MM(pb[:, 0:T], esel[:], zlo[:], False, True, [esel, zlo], [pb])
                yield
                rc = Tf.get()
                ACT(rc[0:64, :], pb[0:64, 0:T], AF.Ln, [pb], [rc])
                yield
                ACT(rc[0:64, :], rc[0:64, :], AF.Exp, [rc], [rc], scale=-1.0)
                yield
                ch, off = h // 2, (h % 2) * 64
                po_sb = Tf.get()
                TT("dve", po_sb[0:64, :], po[hh][0:64, 0:T], rc[0:64, :], ALU.mult, [po[hh], rc], [po_sb])
                yield
                if off:
                    sh = Tf.get()
                    CP("act", sh[off:off + 64, :], po_sb[0:64, :], [po_sb], [sh])
                    yield
                else:
                    sh = po_sb
                STT(yf[off:off + 64, 3 + ch, :], sh[off:off + 64, :], 0.5, sgb[off:off + 64, ch, :], ALU.mult, ALU.mult, [sh, sgb], [yfB])
                yield

            if hg == 0:
                run_rr([g_norm(hh) for hh in range(3)], 3, [Tf, PG])
            else:
                norm_gens = [g_norm(hh) for hh in range(3)]
        def g_rms(ch0, nch, nfeat):
            yfX, yTX = (yfA, yT) if ch0 == 0 else (yfB, yT_b)
            pss = PG.get()
            for i in range(nch):
                sq = Tb.get()
                TT("pool", sq[:], yf[:, ch0 + i, :], yf[:, ch0 + i, :], ALU.mult, [yfX], [sq])
                MM(pss[:, 0:T], ones[:], sq[:], i == 0, i == nch - 1, [ones, sq], [pss])
            yield
            rs = Tf.get()
            ACT(rs[:], pss[:, 0:T], AF.Ln, [pss], [rs], scale=1.0 / nfeat, bias=EPS)
            yield
            ACT(rs[:], rs[:], AF.Exp, [rs], [rs], scale=-0.5)
            yield
            for i in range(nch):
                TT("pool" if i == 0 else "dve", yT[:, ch0 + i, :], yf[:, ch0 + i, :], rs[:], ALU.mult, [yfX, rs], [yTX])
            yield

        psm = [ps_o[0], ps_o[1]]

        def g_sgu(b):
            gv = gvb[:, b, :]
            vnA, vnB = vnA2[b], vnB2[b]
            st2 = st_pool.get()
            S.op("dve", lambda: nv.bn_stats(out=st2[:, 0:6], in_=gv), [gvb], [st2])
            S.op("dve", lambda: nv.bn_aggr(out=st2[:, 8:10], in_=st2[:, 0:6]), [st2], [st2])
            yield
            ACT(st2[:, 10:11], st2[:, 9:10], AF.Ln, [st2], [st2], scale=0.25, bias=EPS)
            yield
            ACT(st2[:, 10:11], st2[:, 10:11], AF.Exp, [st2], [st2], scale=-0.5)
            yield
            vn_ = Tf.get()
            TS("dve", vn_[:, 0:256], gv, st2[:, 8:9], st2[:, 10:11], ALU.subtract, ALU.mult, [gvb, st2], [vn_])
            STT(vn_[:, 0:256], vn_[:, 0:256], 0.5, sgn_g[:], ALU.mult, ALU.mult, [vn_, sgn_g], [vn_])
            yield
            g3 = vn_[:, 0:256].rearrange("p (g c) -> p g c", c=64)
            b3 = sgn_b[:, :].rearrange("p (g c) -> p g c", c=64)
            for par, vn in ((0, vnA), (1, vnB)):
                v3 = vn[:, :].rearrange("p (g c) -> p g c", c=64)
                for cc in range(2):
                    gi = 2 * cc + par
                    TT("pool", v3[:, gi, :], g3[:, gi, :], b3[:, gi, :], ALU.add, [vn_, sgn_b], [vn])
            yield
            for cc in range(2):
                MM(psm[cc][:, b * 128:(b + 1) * 128], vnA[:, cc * 128:(cc + 1) * 128], wsT[:, 2 * cc, :], True, False,
                   [vnA, wsT], [psm[cc]])
                MM(psm[cc][:, b * 128:(b + 1) * 128], vnB[:, cc * 128:(cc + 1) * 128], wsT[:, 2 * cc + 1, :], False, True,
                   [vnB, wsT], [psm[cc]])
            yield

        extra = []
        if nxt is not None:
            result["next_xt"], extra = nxt()
        run_rr(norm_gens + [g_sgu(b) for b in range(NB)] + [g_rms(0, 3, 384)] + extra, 8, [Tf, PG, st_pool, hb_pool, Tb], stagger=True)
        def g_rmsC():
            tms = [Tf.get(), Tf.get()]
            for cc in range(2):
                TT("dve", tms[cc][:], psm[cc][:, 0:T], bsbc[:, cc, :], ALU.add, [psm[cc], bsbc], [tms[cc]])
            yield
            for cc in range(2):
                TT("dve", yf[:, 6 + cc, :], tms[cc][:], ugc[:, cc, :], ALU.mult, [tms[cc], ugc], [yfC])
            yield
            pss = ps_s[1]
            for i in range(2):
                sq = Tb.get()
                TT("pool", sq[:], yf[:, 6 + i, :], yf[:, 6 + i, :], ALU.mult, [yfC], [sq])
                MM(pss[:, 0:T], ones[:], sq[:], i == 0, i == 1, [ones, sq], [pss])
            yield
            rs = Tf.get()
            ACT(rs[:], pss[:, 0:T], AF.Ln, [pss], [rs], scale=1.0 / 256.0, bias=EPS)
            yield
            ACT(rs[:], rs[:], AF.Exp, [rs], [rs], scale=-0.5)
            yield
            for i in range(2):
                TT("dve", yT_c[i][:, 6 + i, :], yf[:, 6 + i, :], rs[:], ALU.mult, [yfC, rs], [yT_c[i]])
            yield

        OB = [ps_o[2], ps_g[0], ps_g[1], ps_s[0]]
        assert NB <= 2

        def g_out(b):
            pso = [OB[2 * b], OB[2 * b + 1]]
            st3 = st_pool.get()
            for half in range(2):
                for c in range(3):
                    MM(pso[half][:, :], yT[:, c, b * 128:(b + 1) * 128], wout[:, c, half * 512:(half + 1) * 512], c == 0, False,
                       [yT, wout], [pso[half]])
            yield
            yield
            for half in range(2):
                for c in range(3, 6):
                    MM(pso[half][:, :], yT[:, c, b * 128:(b + 1) * 128], wout[:, c, half * 512:(half + 1) * 512], False, False,
                       [yT_b, wout], [pso[half]])
            yield
            yield
            yield
            for half in range(2):
                for c in range(6, 8):
                    MM(pso[half][:, :], yT[:, c, b * 128:(b + 1) * 128], wout[:, c, half * 512:(half + 1) * 512], False, c == 7,
                       [yT_c[c - 6], wout], [pso[half]])
            yield
            for half in range(2):
                junk = hb_pool.get()
                ACT(junk[:, 0:512], pso[half][:, :], AF.Square, [pso[half]], [junk, st3], accum_out=st3[:, half:half + 1])
            yield
            TT("dve", st3[:, 2:3], st3[:, 0:1], st3[:, 1:2], ALU.add, [st3], [st3])
            yield
            ACT(st3[:, 3:4], st3[:, 2:3], AF.Ln, [st3], [st3], scale=1.0 / D, bias=EPS)
            yield
            ACT(st3[:, 3:4], st3[:, 3:4], AF.Exp, [st3], [st3], scale=-0.5)
            yield
            for half in range(2):
                xs_ = xt[:, b * D + half * 512:b * D + (half + 1) * 512]
                o1 = o1_pool.get()
                STT(o1[:], pso[half][:, :], st3[:, 3:4], gpost[:, half * 512:(half + 1) * 512], ALU.mult, ALU.mult,
                    [pso[half], st3, gpost], [o1])
                TT("dve", xs_, xs_, o1[:], ALU.add, [xt, o1], [xt])
                yield

        run_rr([g_rms(3, 3, 384), g_rmsC()] + [g_out(b) for b in range(NB)], NB + 2, [Tf, Tb, st_pool, PG], stagger=True)
        S.dma("pool", dst_ap_fn(j).rearrange("(b p) d -> p b d", p=128), xt[:, :].rearrange("p (b d) -> p b d", b=NB),
              reads=[xt], writes=[dst_reg], sembuf=xt)
        return result["next_xt"]

    o1_pool = Rot([S.sbuf("o1_%d" % i, [128, 512], F32) for i in range(3)])

    OVERLAP_P0 = True
    outregs = []
    for l in range(NL):
        prep_layer(l)
        state["hs_prev"] = None
        for c3 in range(3):
            MSET("dve", xa_sb3[c3][:], 0.0, [xa_sb3[c3]])
        src_t = x_in if l == 0 else xs_d[(l - 1) % 2]
        dst_t = out_d if l == NL - 1 else xs_d[l % 2]
        pre = None
        srcf = (lambda jj, st=src_t: st[jj * T:(jj + 1) * T, :])
        for j in range(NT):
            sreg = x_in if l == 0 else xr((l - 1, j))
            dreg = xr((l, j))
            if l == NL - 1:
                outregs.append(dreg)
            nxt = None
            if j + 1 < NT and OVERLAP_P0:
                sreg_n = x_in if l == 0 else xr((l - 1, j + 1))
                nxt = (lambda jn=j + 1, sr=sreg_n: p0_start(jn, srcf, sr))
            pre = tile_layer(l, j, srcf, sreg,
                             (lambda jj, dt=dst_t: dt[jj * T:(jj + 1) * T, :]), dreg, kvwrite=(j < NT - 1), pre=pre, nxt=nxt)
    S.finish(outregs, "sp")
    S.finish(outregs, "pool")
    return nc, S


def inv_freq_table():
    half = 16
    f = (10000.0 ** (-np.arange(half, dtype=np.float32) / half)).astype(np.float32)
    t = np.zeros((96, 1), np.float32)
    t[64:80, 0] = f
    t[80:96, 0] = f
    return t


WNAMES = ["pre_norm_g", "w_in", "conv_w", "conv_b", "lru_wa", "lru_ba", "lru_wx", "lru_bx", "lru_lambda", "q_norm_g",
          "w_uq", "kv_norm_g", "w_ukv", "sgu_norm_g", "sgu_norm_b", "sgu_w", "sgu_b", "branch_norm_g", "w_out",
          "post_norm_g"]

T_TILE = 256


def kernel(**inputs):
    x = np.ascontiguousarray(np.asarray(inputs["x"], dtype=np.float32))
    pos = np.ascontiguousarray(np.asarray(inputs["positions"], dtype=np.int32))
    B, SEQ, _ = x.shape
    NT = SEQ // T_TILE
    nc, S = build(NT, 4, T_TILE)
    wmap = {nm: np.ascontiguousarray(np.asarray(inputs[nm], dtype=np.float32)) for nm in WNAMES}
    invf = inv_freq_table()
    in_maps = []
    NCORES = B
    for core in range(NCORES):
        b = core % B
        m = {"x": x[b], "positions": pos[b].reshape(1, SEQ), "invf": invf}
        m.update(wmap)
        in_maps.append(m)
    res = run_bass_kernel_spmd(nc, in_maps, core_ids=list(range(NCORES)))
    out = np.stack([np.asarray(res.results[b]["out"], dtype=np.float32) for b in range(B)], axis=0)
    return out
```
